# Optimizing a Trainium2 kernel written in Bass

```python
import jax, jax.numpy as jnp
from jax import lax
import numpy as np

D_MODEL = 1024
BATCH = 8
SEQ = 2048
DEPTH = 1
DEC_BATCH = 4
DEC_SEQ = 4096
PAST_LEN = 128

N_MEM = 256
A_HEADS = 8
A_KV_HEADS = 2
A_GROUP = A_HEADS // A_KV_HEADS
A_HEAD_DIM = 64
A_HALF_WIN = 128
B_GROUPS = ((128, 1), (512, 4), (2048, 16))
B_N_GROUPS = 3
B_HEADS_PER_GROUP = 4
B_HEAD_DIM = 128
M_HEADS = 4
M_HEAD_DIM = 128
N_BRANCH = 3
BRANCH_WIDTH = D_MODEL // 2
D_FF = 4 * D_MODEL
EPS = 1e-6
NEG_INF = -1e30

A_Q_W = A_HEADS * A_HEAD_DIM
A_KV_W = A_KV_HEADS * A_HEAD_DIM
B_W = B_N_GROUPS * B_HEADS_PER_GROUP * B_HEAD_DIM
M_Q_W = M_HEADS * M_HEAD_DIM
GATE_W = N_BRANCH * D_MODEL
PROJ_WIDTHS = (A_Q_W, A_KV_W, A_KV_W, B_W, B_W, B_W, M_Q_W, GATE_W)
IN_WIDTH = A_Q_W + 2 * A_KV_W + 3 * B_W + M_Q_W + GATE_W

kernel_name = 'hybrid_gated_parallel_encoder'


def rms_norm(x, gain):
    xf = x.astype(jnp.float32)
    y = xf * lax.rsqrt(jnp.mean(xf * xf, axis=-1, keepdims=True) + EPS) * gain.astype(jnp.float32)
    return y.astype(x.dtype)


def alibi_slopes(n):
    return jnp.asarray(2.0 ** (-8.0 * (np.arange(n) + 1) / n), dtype=jnp.float32)


def banded_attention(q, k, v, slopes, half_win, pos_stride, sink=None):
    n, length, hkv, grp, hd = q.shape
    block = half_win
    nb = -(-length // block)
    lp = nb * block
    extra = lp - length
    qb = jnp.pad(q, ((0, 0), (0, extra), (0, 0), (0, 0), (0, 0))).reshape(n, nb, block, hkv, grp, hd)
    kv_pad = ((0, 0), (block, block + extra), (0, 0), (0, 0))

    def windows(t):
        tb = jnp.pad(t, kv_pad).reshape(n, nb + 2, block, hkv, hd)
        return jnp.concatenate([tb[:, :-2], tb[:, 1:-1], tb[:, 2:]], axis=2)

    kw, vw = windows(k), windows(v)
    q_pos = jnp.arange(nb)[:, None] * block + jnp.arange(block)[None, :]
    k_pos = jnp.arange(nb)[:, None] * block - block + jnp.arange(3 * block)[None, :]
    dist = jnp.abs(q_pos[:, :, None] - k_pos[:, None, :])
    valid = (k_pos[:, None, :] >= 0) & (k_pos[:, None, :] < length) & (dist <= half_win)
    logits = jnp.einsum('nbqhgd,nbkhd->nbhgqk', qb, kw, preferred_element_type=jnp.float32) * (hd ** -0.5)
    penalty = slopes[None, :, :, None, None] * (dist * pos_stride).astype(jnp.float32)[:, None, None]
    logits = jnp.where(valid[:, None, None], logits - penalty, NEG_INF)
    m = jnp.max(logits, axis=-1)
    if sink is not None:
        sink_f = sink.astype(jnp.float32)[None, None, :, :, None]
        m = jnp.maximum(m, sink_f)
    p = jnp.exp(logits - m[..., None])
    denom = jnp.sum(p, axis=-1)
    if sink is not None:
        denom = denom + jnp.exp(sink_f - m)
    out = jnp.einsum('nbhgqk,nbkhd->nbhgqd', p, vw.astype(jnp.float32)) / denom[..., None]
    out = out.transpose(0, 1, 4, 2, 3, 5).reshape(n, lp, hkv, grp, hd)[:, :length]
    lse = (m + jnp.log(denom)).transpose(0, 1, 4, 2, 3).reshape(n, lp, hkv, grp)[:, :length]
    return out, lse


def dilated_attention(q, k, v, slopes):
    n, s, _, hb, hd = q.shape
    outs, lses = [], []
    for gi, (window, dil) in enumerate(B_GROUPS):
        sub = s // dil

        def gather(t):
            return t[:, :, gi].reshape(n, sub, dil, hb, hd).transpose(0, 2, 1, 3, 4).reshape(n * dil, sub, hb, hd)

        o, lse = banded_attention(gather(q)[:, :, :, None], gather(k), gather(v),
                                  slopes[gi][:, None], window // (2 * dil), dil)
        outs.append(o[:, :, :, 0].reshape(n, dil, sub, hb, hd).transpose(0, 2, 1, 3, 4).reshape(n, s, hb, hd))
        lses.append(lse[:, :, :, 0].reshape(n, dil, sub, hb).transpose(0, 2, 1, 3).reshape(n, s, hb))
    weights = jax.nn.softmax(jnp.stack(lses), axis=0)
    return jnp.einsum('gnsh,gnshd->nshd', weights, jnp.stack(outs))


def memory_attention(q, k, v):
    logits = jnp.einsum('nshd,nmhd->nhsm', q, k, preferred_element_type=jnp.float32) * (q.shape[-1] ** -0.5)
    p = jax.nn.softmax(logits, axis=-1)
    return jnp.einsum('nhsm,nmhd->nshd', p, v.astype(jnp.float32))


def encoder_layer(x, mem, g_mix, g_mem, w_in, b_gate, w_mem_kv, gq_a, gk_a, sink_a,
                  gq_b, gk_b, gq_m, gk_m, w_branch, w_out, g_mlp, w_up, w_down):
    n, s, _ = x.shape
    h = rms_norm(x, g_mix)
    proj = h @ w_in
    split_at = [int(i) for i in np.cumsum(PROJ_WIDTHS)[:-1]]
    qa, ka, va, qb, kb, vb, qm, gate_logits = jnp.split(proj, split_at, axis=-1)

    qa = rms_norm(qa.reshape(n, s, A_KV_HEADS, A_GROUP, A_HEAD_DIM), gq_a)
    ka = rms_norm(ka.reshape(n, s, A_KV_HEADS, A_HEAD_DIM), gk_a)
    va = va.reshape(n, s, A_KV_HEADS, A_HEAD_DIM)
    oa, _ = banded_attention(qa, ka, va, alibi_slopes(A_HEADS).reshape(A_KV_HEADS, A_GROUP),
                             A_HALF_WIN, 1, sink=sink_a.reshape(A_KV_HEADS, A_GROUP))
    oa = oa.reshape(n, s, A_Q_W)

    shp_b = (n, s, B_N_GROUPS, B_HEADS_PER_GROUP, B_HEAD_DIM)
    qb = rms_norm(qb.reshape(shp_b), gq_b)
    kb = rms_norm(kb.reshape(shp_b), gk_b)
    vb = vb.reshape(shp_b)
    ob = dilated_attention(qb, kb, vb, alibi_slopes(B_N_GROUPS * B_HEADS_PER_GROUP).reshape(B_N_GROUPS, B_HEADS_PER_GROUP))
    ob = ob.reshape(n, s, B_HEADS_PER_GROUP * B_HEAD_DIM)

    mh = rms_norm(mem, g_mem)
    mk, mv = jnp.split(mh @ w_mem_kv, 2, axis=-1)
    qm = rms_norm(qm.reshape(n, s, M_HEADS, M_HEAD_DIM), gq_m)
    mk = rms_norm(mk.reshape(n, N_MEM, M_HEADS, M_HEAD_DIM), gk_m)
    mv = mv.reshape(n, N_MEM, M_HEADS, M_HEAD_DIM)
    om = memory_attention(qm, mk, mv).reshape(n, s, M_HEADS * M_HEAD_DIM)

    branches = jnp.stack([oa, ob, om], axis=2).astype(x.dtype)
    y_br = jnp.einsum('nsbc,bcd->nsbd', branches, w_branch)
    gates = jax.nn.sigmoid(gate_logits.reshape(n, s, N_BRANCH, D_MODEL) + b_gate)
    x = x + jnp.sum(gates * y_br, axis=2) @ w_out

    h2 = rms_norm(x, g_mlp)
    x = x + jnp.square(jax.nn.relu(h2 @ w_up)) @ w_down
    return x


def setup_inputs(seed: int = 0) -> dict:
    key = jax.random.key(seed)
    ks = jax.random.split(key, 24)
    f32 = jnp.float32

    def nrm(k, shape, scale):
        return jax.random.normal(k, shape, f32) * scale

    def gain(k, shape):
        return 1.0 + 0.05 * jax.random.normal(k, shape, f32)

    return {
        'x_prompt': nrm(ks[0], (BATCH, SEQ, D_MODEL), 1.0),
        'x_sample': nrm(ks[1], (DEC_BATCH, DEC_SEQ, D_MODEL), 1.0),
        'mem_prompt': nrm(ks[2], (BATCH, N_MEM, D_MODEL), 1.0),
        'mem_sample': nrm(ks[3], (DEC_BATCH, N_MEM, D_MODEL), 1.0),
        'g_mix': gain(ks[4], (DEPTH, D_MODEL)),
        'g_mem': gain(ks[5], (DEPTH, D_MODEL)),
        'w_in': nrm(ks[6], (DEPTH, D_MODEL, IN_WIDTH), D_MODEL ** -0.5),
        'b_gate': nrm(ks[7], (DEPTH, N_BRANCH, D_MODEL), 0.1),
        'w_mem_kv': nrm(ks[8], (DEPTH, D_MODEL, 2 * M_HEADS * M_HEAD_DIM), D_MODEL ** -0.5),
        'gq_a': gain(ks[9], (DEPTH, A_HEAD_DIM)),
        'gk_a': gain(ks[10], (DEPTH, A_HEAD_DIM)),
        'sink_a': nrm(ks[11], (DEPTH, A_HEADS), 0.5),
        'gq_b': gain(ks[12], (DEPTH, B_HEAD_DIM)),
        'gk_b': gain(ks[13], (DEPTH, B_HEAD_DIM)),
        'gq_m': gain(ks[14], (DEPTH, M_HEAD_DIM)),
        'gk_m': gain(ks[15], (DEPTH, M_HEAD_DIM)),
        'w_branch': nrm(ks[16], (DEPTH, N_BRANCH, BRANCH_WIDTH, D_MODEL), BRANCH_WIDTH ** -0.5),
        'w_out': nrm(ks[17], (DEPTH, D_MODEL, D_MODEL), D_MODEL ** -0.5),
        'g_mlp': gain(ks[18], (DEPTH, D_MODEL)),
        'w_up': nrm(ks[19], (DEPTH, D_MODEL, D_FF), D_MODEL ** -0.5),
        'w_down': nrm(ks[20], (DEPTH, D_FF, D_MODEL), D_FF ** -0.5),
    }


def reference(x_prompt, x_sample, mem_prompt, mem_sample, g_mix, g_mem, w_in, b_gate, w_mem_kv,
              gq_a, gk_a, sink_a, gq_b, gk_b, gq_m, gk_m, w_branch, w_out, g_mlp, w_up, w_down):
    def run(x, mem):
        for l in range(DEPTH):
            x = encoder_layer(x, mem, g_mix[l], g_mem[l], w_in[l], b_gate[l], w_mem_kv[l],
                              gq_a[l], gk_a[l], sink_a[l], gq_b[l], gk_b[l], gq_m[l], gk_m[l],
                              w_branch[l], w_out[l], g_mlp[l], w_up[l], w_down[l])
        return x

    y_prompt = run(x_prompt, mem_prompt)
    y_sample = run(x_sample, mem_sample)
    return (y_prompt, y_sample)
```

```python
import numpy as np
import ml_dtypes
from contextlib import ExitStack
import concourse.bass as bass
import concourse.mybir as mybir
from concourse.bass_utils import run_bass_kernel_spmd

F32 = mybir.dt.float32
BF16 = mybir.dt.bfloat16
AF = mybir.ActivationFunctionType
ALU = mybir.AluOpType
NPBF = ml_dtypes.bfloat16

NT = 4096
D = 1024
NMEM = 512
EPS = 1e-6
TT = 512
NTT = NT // TT
B_DIL = (1, 4, 16)

C_QA, C_KA, C_VA, C_QB, C_KB, C_VB, C_QM, C_GT = 0, 512, 640, 768, 2304, 3840, 5376, 5888

P_GMIX, P_GMEM, P_GMLP, P_BG, P_GQA, P_GKA, P_GQB, P_GKB, P_GQM, P_GKM, P_SINK, P_NEGB, P_SINKE, P_N = \
    0, 8, 16, 24, 48, 49, 50, 51, 52, 53, 54, 58, 82, 86

SHIFT_A = 8.0
SHIFT_B = 11.5
SAME_ENG_SYNC = True


class Buf:
    __slots__ = ("name", "w", "r", "slot")

    def __init__(self, name):
        self.name = name
        self.w = None
        self.r = {}
        self.slot = None


class Sem:
    _n = 0

    def __init__(self, h):
        self.h = h
        Sem._n += 1
        self.key = Sem._n


class Eng:
    def __init__(self, name, h, sem):
        self.name, self.h, self.sem = name, h, sem
        self.count = 0
        self.waited = {}


class Slot:
    def __init__(self, sem):
        self.sem = sem
        self.count = 0


class Ctx:
    def __init__(self, nc, es):
        self.nc = nc
        self.es = es
        mk = lambda n: Sem(es.enter_context(nc.semaphore(n)))
        self.pe = Eng("pe", nc.tensor, mk("s_pe"))
        self.act = Eng("act", nc.scalar, mk("s_act"))
        self.dve = Eng("dve", nc.vector, mk("s_dve"))
        self.pool = Eng("pool", nc.gpsimd, mk("s_pool"))
        self.sp = Eng("sp", nc.sync, mk("s_sp"))
        self.engs = [self.pe, self.act, self.dve, self.pool, self.sp]
        self.slots = []
        self.nslot = 0

    def slot(self):
        return None

    def _buf_slot(self, b):
        if b.slot is None:
            self.nslot += 1
            b.slot = Slot(Sem(self.es.enter_context(self.nc.semaphore("s_dma%d" % self.nslot))))
            self.slots.append(b.slot)
        return b.slot

    def _wait(self, e, tok):
        sem, val, owner = tok
        if owner is e and (e.name == "pe" or not SAME_ENG_SYNC):
            return
        if e.waited.get(sem.key, 0) >= val:
            return
        e.h.wait_ge(sem.h, val)
        e.waited[sem.key] = val

    def _deps(self, e, reads, writes):
        toks = []
        for b in reads:
            if b.w is not None:
                toks.append(b.w)
        for b in writes:
            if b.w is not None:
                toks.append(b.w)
            toks.extend(b.r.values())
        for t in toks:
            self._wait(e, t)

    def _commit(self, tok, reads, writes):
        for b in reads:
            old = b.r.get(tok[0].key)
            if old is None or old[1] < tok[1]:
                b.r[tok[0].key] = tok
        for b in writes:
            b.w = tok
            b.r = {}

    def op(self, e, fn, reads=(), writes=()):
        self._deps(e, reads, writes)
        inst = fn(e.h)
        e.count += 1
        inst.then_inc(e.sem.h, 1)
        tok = (e.sem, e.count, e)
        self._commit(tok, reads, writes)
        return tok

    def dma(self, e, slot, out, in_, reads=(), writes=()):
        slot = self._buf_slot((list(writes) + list(reads))[0])
        self._deps(e, reads, writes)
        inst = e.h.dma_start(out=out, in_=in_)
        slot.count += 16
        inst.then_inc(slot.sem.h, 16)
        tok = (slot.sem, slot.count, None)
        self._commit(tok, reads, writes)
        return tok

    def barrier(self):
        for e in self.engs:
            for f in self.engs:
                if f is not e and f.count > 0:
                    self._wait(e, (f.sem, f.count, f))
            for s in self.slots:
                if s.count > 0:
                    self._wait(e, (s.sem, s.count, None))


def build_program(debug=False):
    nc = bass.Bass("TRN2", target_bir_lowering=False)
    dt_in = lambda n, s, d=F32: nc.dram_tensor(n, s, d, kind="ExternalInput").ap()
    skind = "ExternalOutput" if debug else "Internal"
    dt_s = lambda n, s, d=BF16: nc.dram_tensor(n, s, d, kind=skind).ap()

    x_d = dt_in("x", [NT, D])
    mem_d = dt_in("mem", [NMEM, D])
    w_in_d = dt_in("w_in", [D, 8960])
    w_mkv_d = dt_in("w_mem_kv", [D, 1024])
    w_br_d = dt_in("w_branch", [1536, D])
    w_out_d = dt_in("w_out", [D, D])
    w_up_d = dt_in("w_up", [D, 4096])
    w_dn_d = dt_in("w_down", [4096, D])
    par_d = dt_in("params", [128, P_N])
    ident_d = dt_in("ident", [128, 128], BF16)
    wa_d = dt_in("wa_tab", [128, 3, 8, 384], BF16)
    wb_d = dt_in("wb_tab", [128, 3, 12, 256], BF16)
    y_d = nc.dram_tensor("y", [NT, D], F32, kind="ExternalOutput").ap()

    QA = dt_s("sQA", [4, 128, NT])
    KA = dt_s("sKA", [128, NT])
    QB = dt_s("sQB", [12, 128, NT])
    KB = dt_s("sKB", [12, 128, NT])
    QM = dt_s("sQM", [4, 128, NT])
    GT = dt_s("sGT", [24, 128, NT])
    VA = dt_s("sVA", [NT, 256])
    VB = dt_s("sVB", [3, NT, 512])
    OT = dt_s("sOT", [12, 128, NT])
    H2T = dt_s("sH2T", [8, 128, NT])

    with ExitStack() as es:
        es.enter_context(nc.allow_low_precision(reason="bf16 matmul operands, fp32 accumulation"))
        cx = Ctx(nc, es)
        pe, act, dve, pool, sp = cx.pe, cx.act, cx.dve, cx.pool, cx.sp
        sb = lambda st, n, s, d: st.enter_context(nc.sbuf_tensor("t_" + n, s, d))
        ps = lambda st, n, s, d: st.enter_context(nc.psum_tensor("p_" + n, s, d))

        par = sb(es, "par", [128, P_N], F32)
        ident = sb(es, "ident", [128, 128], BF16)
        ones128 = sb(es, "ones128", [128, 128], BF16)
        onesblk = sb(es, "onesblk", [128, 128], BF16)
        pw3 = ExitStack()
        wbr = sb(pw3, "wbr", [128, 12, D], BF16)
        wout = sb(pw3, "wout", [128, 8, D], BF16)
        b_wbr, b_wout = Buf("wbr"), Buf("wout")
        pm = ExitStack()
        MKT = sb(pm, "MKT", [128, 4, NMEM], BF16)
        MV = sb(pm, "MV", [128, 4, 512], BF16)
        b_par, b_ident, b_ones, b_mkt, b_mv = Buf("par"), Buf("ident"), Buf("ones"), Buf("mkt"), Buf("mv")
        sl_c = cx.slot()
        cx.dma(sp, sl_c, par[:, 0:P_NEGB], par_d[:, 0:P_NEGB], writes=[b_par])
        cx.dma(sp, sl_c, ident[:], ident_d[:, :], writes=[b_ident])
        cx.op(dve, lambda h: h.memset(ones128[:], 1.0), writes=[b_ones])
        cx.op(dve, lambda h: h.memset(onesblk[:], 0.0), writes=[b_ones])
        cx.op(dve, lambda h: h.memset(onesblk[0:64, 0:64], 1.0), writes=[b_ones])
        cx.op(dve, lambda h: h.memset(onesblk[64:128, 64:128], 1.0), writes=[b_ones])
        cx.op(dve, lambda h: h.tensor_scalar(out=par[:, P_NEGB:P_NEGB + 24], in0=par[:, P_BG:P_BG + 24],
                                             scalar1=-1.0, scalar2=None, op0=ALU.mult), reads=[b_par], writes=[b_par])
        cx.op(act, lambda h: h.activation(out=par[:, P_SINKE:P_SINKE + 4], in_=par[:, P_SINK:P_SINK + 4],
                                          func=AF.Exp, bias=-SHIFT_A), reads=[b_par], writes=[b_par])

        p01 = ExitStack()
        hT = sb(p01, "hT", [128, 8, NT], BF16)
        mhT = sb(p01, "mhT", [128, 8, NMEM], BF16)
        b_hT = [Buf("hT%d" % t) for t in range(NTT)]
        b_mhT = Buf("mhT")

        p0 = ExitStack()
        NXT = 3
        xt = [sb(p0, "xt%d" % i, [128, 4, D], F32) for i in range(NXT)]
        b_xt = [Buf("xt%d" % i) for i in range(NXT)]
        sl_xt = [cx.slot(), cx.slot()]
        xn = [sb(p0, "xn%d" % i, [128, D], BF16) for i in range(2)]
        b_xn = [Buf("xn0"), Buf("xn1")]
        sqj = sb(p0, "sqj", [128, D], BF16)
        b_sqj = Buf("sqj")
        stt = sb(p0, "stt", [128, 3, 36], F32)
        b_st = [Buf("st%d" % j) for j in range(36)]
        pT = [ps(p0, "pT%d" % i, [128, 8, 128], BF16) for i in range(2)]
        b_pT = [Buf("pT0"), Buf("pT1")]

        def x_src(t):
            if t < NTT:
                return x_d[t * TT:(t + 1) * TT, :].rearrange("(s p) d -> p s d", p=128)
            return mem_d[:, :].rearrange("(s p) d -> p s d", p=128)

        cx.dma(sp, None, xt[0][:], x_src(0), writes=[b_xt[0]])
        cx.dma(sp, None, xt[1][:], x_src(1), writes=[b_xt[1]])
        for t in range(NTT + 1):
            if t + 2 <= NTT:
                cx.dma(sp, None, xt[(t + 2) % NXT][:], x_src(t + 2), writes=[b_xt[(t + 2) % NXT]])
            xb, bxb = xt[t % NXT], b_xt[t % NXT]
            bst = b_st[t]
            for s in range(4):
                j = t * 4 + s
                cx.op(act, lambda h: h.activation(out=sqj[:], in_=xb[:, s, :], func=AF.Square,
                                                  accum_out=stt[:, 0, j:j + 1]),
                      reads=[bxb], writes=[b_sqj, bst])
            cx.op(act, lambda h: h.activation(out=stt[:, 1, 4 * t:4 * t + 4], in_=stt[:, 0, 4 * t:4 * t + 4], func=AF.Ln,
                                              scale=1.0 / D, bias=EPS), reads=[bst], writes=[bst])
            cx.op(act, lambda h: h.activation(out=stt[:, 2, 4 * t:4 * t + 4], in_=stt[:, 1, 4 * t:4 * t + 4], func=AF.Exp,
                                              scale=-0.5), reads=[bst], writes=[bst])
            for s in range(4):
                j = t * 4 + s
                xnb, bxn = xn[j % 2], b_xn[j % 2]
                ptb, bpt = pT[j % 2], b_pT[j % 2]
                if s % 2 == 0:
                    cx.op(act, lambda h: h.activation(out=xnb[:], in_=xb[:, s, :], func=AF.Copy,
                                                      scale=stt[:, 2, j:j + 1]), reads=[bxb, bst], writes=[bxn])
                else:
                    cx.op(dve, lambda h: h.tensor_scalar(out=xnb[:], in0=xb[:, s, :], scalar1=stt[:, 2, j:j + 1],
                                                         scalar2=None, op0=ALU.mult), reads=[bxb, bst], writes=[bxn])

                def tr(h):
                    for kc in range(8):
                        i = h.transpose(out=ptb[:, kc, :], in_=xnb[:, kc * 128:(kc + 1) * 128], identity=ident[:])
                    return i
                cx.op(pe, tr, reads=[bxn, b_ident], writes=[bpt])
                if t < NTT:
                    dst, bd, gcol = hT[:, :, j * 128:(j + 1) * 128], b_hT[t], P_GMIX
                else:
                    dst, bd, gcol = mhT[:, :, s * 128:(s + 1) * 128], b_mhT, P_GMEM
                cx.op(dve, lambda h: h.tensor_tensor(out=dst, in0=ptb[:],
                                                     in1=par[:, gcol:gcol + 8].unsqueeze(2).to_broadcast([128, 8, 128]),
                                                     op=ALU.mult), reads=[bpt, b_par], writes=[bd])
        cx.barrier()
        p0.close()

        p1 = ExitStack()
        NW = 3
        wt = [sb(p1, "wt%d" % i, [128, 8, 512], BF16) for i in range(NW)]
        b_wt = [Buf("wt%d" % i) for i in range(NW)]
        sl_wt = [cx.slot() for _ in range(NW)]
        wmk = sb(p1, "wmk", [128, 8, 1024], BF16)
        b_wmk = Buf("wmk")
        sl_wmk = cx.slot()
        NSTG = 3
        stg = [sb(p1, "stg%d" % i, [128, NT], BF16) for i in range(NSTG)]
        b_stg = [Buf("stg%d" % i) for i in range(NSTG)]
        sl_stg = [cx.slot() for _ in range(NSTG)]
        vst = [sb(p1, "vst%d" % i, [128, 4, 512], BF16) for i in range(2)]
        b_vst = [Buf("vst0"), Buf("vst1")]
        sl_vst = [cx.slot(), cx.slot()]
        sqb = [sb(p1, "sqb%d" % i, [128, TT], BF16) for i in range(2)]
        b_sqb = [Buf("sqb0"), Buf("sqb1")]
        cpb = [sb(p1, "cpb%d" % i, [128, TT], BF16) for i in range(2)]
        b_cpb = [Buf("cpb0"), Buf("cpb1")]
        lnb = [sb(p1, "lnb%d" % i, [128, TT], F32) for i in range(2)]
        b_lnb = [Buf("lnb0"), Buf("lnb1")]
        NA = 4
        acc = [ps(p1, "acc%d" % i, [128, 512], F32) for i in range(NA)]
        b_acc = [Buf("acc%d" % i) for i in range(NA)]
        ssp = [ps(p1, "ssp%d" % i, [128, 512], F32) for i in range(2)]
        b_ssp = [Buf("ssp0"), Buf("ssp1")]

        def wsrc(c0, n):
            return w_in_d[:, c0:c0 + n].rearrange("(kc p) c -> p kc c", p=128)

        wtiles = []
        wtiles.append([(0, C_KA, 128),
                       (128, C_VA, 64), (192, C_VA, 64), (256, C_VA + 64, 64), (320, C_VA + 64, 64)])
        wtiles.append([(0, C_QA, 512)])
        for g in range(3):
            wtiles.append([(0, C_KB + 512 * g, 512)])
        for g in range(3):
            wtiles.append([(0, C_VB + 512 * g, 512)])
        for g in range(3):
            wtiles.append([(0, C_QB + 512 * g, 512)])
        wtiles.append([(0, C_QM, 512)])
        for i in range(6):
            wtiles.append([(0, C_GT + 512 * i, 512)])

        def load_w(k):
            i = k % NW
            for (dc, c0, n) in wtiles[k]:
                cx.dma(pool, sl_wt[i], wt[i][:, :, dc:dc + n], wsrc(c0, n), writes=[b_wt[i]])

        units = []
        stg_ctr = [0]
        vst_ctr = [0]

        def add_feat(k, wcol, dest, dil, gcol, hd, rhs_t, rhs_b, ntok_tiles, direct=None):
            si = stg_ctr[0] % NSTG
            if direct is None:
                stg_ctr[0] += 1
            for t in range(ntok_tiles):
                units.append(dict(kind="f", k=k, wcol=wcol, dest=dest, dil=dil, gcol=gcol, hd=hd, t=t,
                                  rhs=rhs_t, rhs_b=rhs_b[t] if isinstance(rhs_b, list) else rhs_b,
                                  si=si, last=(t == ntok_tiles - 1), direct=direct))

        def add_gate(k, wcol, j):
            si = stg_ctr[0] % NSTG
            stg_ctr[0] += 1
            for t in range(NTT):
                units.append(dict(kind="g", k=k, wcol=wcol, dest=GT[j], j=j, t=t, rhs=hT, rhs_b=b_hT[t],
                                  si=si, last=(t == NTT - 1)))

        def add_v(k, wcol, ncols, dest_fn):
            for t in range(NTT):
                vi = vst_ctr[0] % 2
                vst_ctr[0] += 1
                for s in range(4):
                    units.append(dict(kind="v", k=k, wcol=wcol, ncols=ncols, t=t, s=s, vi=vi, dest=dest_fn(t),
                                      last=(s == 3)))

        for hh in range(4):
            units.append(dict(kind="f", k=-1, wcol=128 * hh, dest=None, dil=1, gcol=P_GKM, hd=128, t=0,
                              rhs=mhT, rhs_b=b_mhT, si=0, last=False, direct=MKT[:, hh, :]))
        for s in range(4):
            units.append(dict(kind="v", k=-1, wcol=512, ncols=512, t=0, s=s, vi=0, dest=None, last=False,
                              direct=MV[:, s, :]))

        add_feat(0, 0, KA, 1, P_GKA, 64, hT, b_hT, NTT)
        add_v(0, 128, 256, lambda t: VA[t * TT:(t + 1) * TT, :].rearrange("(s p) c -> p s c", p=128))
        k = 1
        for c in range(4):
            add_feat(k, 128 * c, QA[c], 1, P_GQA, 64, hT, b_hT, NTT)
        k += 1
        for g in range(3):
            for hh in range(4):
                add_feat(k, 128 * hh, KB[g * 4 + hh], B_DIL[g], P_GKB, 128, hT, b_hT, NTT)
            k += 1
        for g in range(3):
            add_v(k, 0, 512, (lambda g: lambda t: VB[g, t * TT:(t + 1) * TT, :].rearrange("(s p) c -> p s c", p=128))(g))
            k += 1
        for g in range(3):
            for hh in range(4):
                add_feat(k, 128 * hh, QB[g * 4 + hh], B_DIL[g], P_GQB, 128, hT, b_hT, NTT)
            k += 1
        for hh in range(4):
            add_feat(k, 128 * hh, QM[hh], 1, P_GQM, 128, hT, b_hT, NTT)
        k += 1
        for i in range(6):
            for c in range(4):
                add_gate(k, 128 * c, i * 4 + c)
            k += 1
        NWT = k
        assert NWT == len(wtiles)
        cx.dma(pool, sl_wmk, wmk[:], w_mkv_d[:, :].rearrange("(kc p) c -> p kc c", p=128), writes=[b_wmk])
        load_w(0)
        load_w(1)
        loaded = 2

        def wbuf(u):
            if u["k"] < 0:
                return wmk, b_wmk
            return wt[u["k"] % NW], b_wt[u["k"] % NW]

        def emit_proj(ui, u):
            nonlocal loaded
            kk = u["k"]
            while kk >= 0 and loaded < NWT and loaded <= kk + 2:
                load_w(loaded)
                loaded += 1
            w, bw = wbuf(u)
            a, ba = acc[ui % NA], b_acc[ui % NA]
            t = u["t"]
            if u["kind"] in ("f", "g"):
                rhs = u["rhs"]

                def mm(h):
                    for kc in range(8):
                        i = h.matmul(a[:, :], lhsT=w[:, kc, u["wcol"]:u["wcol"] + 128],
                                     rhs=rhs[:, kc, t * TT:(t + 1) * TT], start=(kc == 0), stop=(kc == 7))
                    return i
                cx.op(pe, mm, reads=[bw, u["rhs_b"]], writes=[ba])
            else:
                src, bsrc = (hT, b_hT[t]) if kk >= 0 else (mhT, b_mhT)
                tok0 = t * TT + u["s"] * 128
                nco = u["ncols"]

                def mm(h):
                    for kc in range(8):
                        i = h.matmul(a[:, 0:nco], lhsT=src[:, kc, tok0:tok0 + 128],
                                     rhs=w[:, kc, u["wcol"]:u["wcol"] + nco], start=(kc == 0), stop=(kc == 7))
                    return i
                cx.op(pe, mm, reads=[bw, bsrc], writes=[ba])

        def emit_first(ui, u):
            a, ba = acc[ui % NA], b_acc[ui % NA]
            if u["kind"] == "f":
                q, bq = sqb[ui % 2], b_sqb[ui % 2]
                c_, bc = cpb[ui % 2], b_cpb[ui % 2]
                cx.op(dve, lambda h: h.tensor_copy(out=c_[:], in_=a[:, :]), reads=[ba], writes=[bc])
                cx.op(dve, lambda h: h.tensor_tensor(out=q[:], in0=c_[:], in1=c_[:], op=ALU.mult),
                      reads=[bc], writes=[bq])
            elif u["kind"] == "g":
                j = u["j"]
                s_, bs = stg[u["si"]], b_stg[u["si"]]
                t = u["t"]
                cx.op(act, lambda h: h.activation(out=s_[:, t * TT:(t + 1) * TT], in_=a[:, :], func=AF.Sigmoid,
                                                  bias=par[:, P_BG + j:P_BG + j + 1]),
                      reads=[ba, b_par], writes=[bs])
            else:
                nco = u["ncols"]
                if u.get("direct") is not None:
                    cx.op(dve, lambda h: h.tensor_copy(out=u["direct"], in_=a[:, 0:nco]), reads=[ba], writes=[b_mv])
                else:
                    v, bv = vst[u["vi"]], b_vst[u["vi"]]
                    cx.op(dve, lambda h: h.tensor_copy(out=v[:, u["s"], 0:nco], in_=a[:, 0:nco]),
                          reads=[ba], writes=[bv])
                    if u["last"]:
                        cx.dma(sp, sl_vst[u["vi"]], u["dest"], v[:, :, 0:nco], reads=[bv])

        def emit_rest(ui, u):
            a, ba = acc[ui % NA], b_acc[ui % NA]
            t = u["t"]
            if u["kind"] == "f":
                q, bq = sqb[ui % 2], b_sqb[ui % 2]
                sp_, bsp = ssp[ui % 2], b_ssp[ui % 2]
                l, bl = lnb[ui % 2], b_lnb[ui % 2]
                om = ones128 if u["hd"] == 128 else onesblk
                cx.op(pe, lambda h: h.matmul(sp_[:, :], lhsT=om[:], rhs=q[:], start=True, stop=True),
                      reads=[bq, b_ones], writes=[bsp])
                cx.op(act, lambda h: h.activation(out=l[:], in_=sp_[:, :], func=AF.Ln, scale=1.0 / u["hd"], bias=EPS),
                      reads=[bsp], writes=[bl])
                cx.op(act, lambda h: h.activation(out=l[:], in_=l[:], func=AF.Exp, scale=-0.5),
                      reads=[bl], writes=[bl])
                d = u["dil"]
                gcol = u["gcol"]
                if u.get("direct") is not None:
                    cx.op(dve, lambda h: h.scalar_tensor_tensor(out=u["direct"], in0=a[:, :], scalar=par[:, gcol:gcol + 1],
                                                                in1=l[:], op0=ALU.mult, op1=ALU.mult),
                          reads=[ba, bl, b_par], writes=[b_mkt])
                    return
                s_, bs = stg[u["si"]], b_stg[u["si"]]
                if d == 1:
                    o_ap, a_ap, l_ap = s_[:, t * TT:(t + 1) * TT], a[:, :], l[:]
                else:
                    npl = TT // d
                    o_ap = s_[:, :].rearrange("p (r pos) -> p r pos", r=d)[:, :, t * npl:(t + 1) * npl]
                    a_ap = a[:, :].rearrange("p (pl r) -> p r pl", r=d)
                    l_ap = l[:, :].rearrange("p (pl r) -> p r pl", r=d)
                cx.op(dve, lambda h: h.scalar_tensor_tensor(out=o_ap, in0=a_ap, scalar=par[:, gcol:gcol + 1],
                                                            in1=l_ap, op0=ALU.mult, op1=ALU.mult),
                      reads=[ba, bl, b_par], writes=[bs])
                if u["last"]:
                    cx.dma(sp, sl_stg[u["si"]], u["dest"], s_[:], reads=[bs])
            elif u["kind"] == "g":
                s_, bs = stg[u["si"]], b_stg[u["si"]]
                if u["last"]:
                    cx.dma(sp, sl_stg[u["si"]], u["dest"], s_[:], reads=[bs])

        for ui in range(len(units) + 1):
            if ui < len(units):
                emit_proj(ui, units[ui])
                emit_first(ui, units[ui])
            if ui >= 1:
                emit_rest(ui - 1, units[ui - 1])
        cx.barrier()
        p1.close()
        p01.close()

        if debug == 1:
            pm.close()
            pw3.close()
            return nc

        for i in range(3):
            cx.dma(pool, None, wbr[:, 4 * i:4 * i + 4, :],
                   w_br_d[512 * i:512 * (i + 1), :].rearrange("(kc p) n -> p kc n", p=128), writes=[b_wbr])
        for i in range(2):
            cx.dma(pool, None, wout[:, 4 * i:4 * i + 4, :],
                   w_out_d[512 * i:512 * (i + 1), :].rearrange("(kc p) n -> p kc n", p=128), writes=[b_wout])
        p2 = ExitStack()
        NS = 4
        sbank = [ps(p2, "S%d" % i, [128, 512], F32) for i in range(NS)]
        b_sbank = [Buf("S%d" % i) for i in range(NS)]
        nbank = [ps(p2, "N%d" % i, [128, 512], F32) for i in range(2)]
        b_nbank = [Buf("N%d" % i) for i in range(2)]
        dbank = [ps(p2, "Dn%d" % i, [128, 512], F32) for i in range(2)]
        b_dbank = [Buf("Dn%d" % i) for i in range(2)]
        ebuf = [sb(p2, "E%d" % i, [128, 512], BF16) for i in range(NS)]
        b_ebuf = [Buf("E%d" % i) for i in range(NS)]
        ptbuf = [sb(p2, "PT%d" % i, [128, 512], BF16) for i in range(NS)]
        b_ptbuf = [Buf("PT%d" % i) for i in range(NS)]
        rD = [sb(p2, "rD%d" % i, [128, 512], F32) for i in range(2)]
        b_rD = [Buf("rD0"), Buf("rD1")]
        ost = [sb(p2, "ost%d" % i, [128, 4, 512], BF16) for i in range(2)]
        b_ost = [Buf("ost0"), Buf("ost1")]
        sl_ost = [cx.slot(), cx.slot()]
        qt_ = [sb(p2, "qt%d" % i, [128, 4, 512], BF16) for i in range(2)]
        b_qt = [Buf("qt0"), Buf("qt1")]
        sl_qt = [cx.slot(), cx.slot()]

        def attn_step(items, i, n, LAG):
            if i < n:
                it = items[i]
                r3 = i % NS
                nq = it["nq"]
                S, bS = sbank[r3], b_sbank[r3]
                if "s_list" in it:
                    def smm(h):
                        for (k_ap, q_ap, off, n1) in it["s_list"]:
                            ins = h.matmul(S[:, off:off + n1], lhsT=k_ap, rhs=q_ap, start=True, stop=True,
                                           skip_group_check=True)
                        return ins
                    cx.op(pe, smm, reads=it["kq_bufs"], writes=[bS])
                else:
                    cx.op(pe, lambda h: h.matmul(it["s_view"](S), lhsT=it["k"], rhs=it["q"], start=True, stop=True),
                          reads=it["kq_bufs"], writes=[bS])
                P_, bP = ptbuf[r3], b_ptbuf[r3]
                if it["mask"] is None:
                    cx.op(act, lambda h: h.activation(out=P_[:, 0:nq], in_=S[:, 0:nq], func=AF.Exp,
                                                      scale=it["scale"], bias=it["shift"]),
                          reads=[bS], writes=[bP])
                else:
                    E_, bE = ebuf[r3], b_ebuf[r3]
                    cx.op(act, lambda h: h.activation(out=E_[:, 0:nq], in_=S[:, 0:nq], func=AF.Exp,
                                                      scale=it["scale"], bias=it["shift"]),
                          reads=[bS], writes=[bE])
                    me = pool if (it.get("pool_share") and i % 4 == 3) else dve
                    cx.op(me, lambda h: h.tensor_tensor(out=it["s_view"](P_), in0=it["s_view"](E_), in1=it["mask"],
                                                        op=ALU.mult),
                          reads=[bE] + it["mask_bufs"], writes=[bP])
            if i >= LAG:
                it = items[i - LAG]
                r3 = (i - LAG) % NS
                P_, bP = ptbuf[r3], b_ptbuf[r3]

                def pv(h):
                    for ent in it["pv"]:
                        if len(ent) == 2:
                            o_ap, v_ap = ent
                            r_ap = it["s_view"](P_)
                        else:
                            o_ap, v_ap, off, n1 = ent
                            r_ap = P_[:, off:off + n1]
                        st_flag = it["start"] and (len(ent) == 2 or off == 0)
                        ins = h.matmul(o_ap, lhsT=v_ap, rhs=r_ap, start=st_flag, stop=it["stop"],
                                       skip_group_check=True)
                    return ins
                for pp in list(pending):
                    if pp[2] is not it.get("grp_id") and pp[3] == it.get("bank"):
                        pending.remove(pp)
                        pp[1]()
                cx.op(pe, pv, reads=[bP] + it["v_bufs"], writes=it["out_bufs"])
                if it.get("post") is not None:
                    pending.append((i + POST_DELAY, it["post"], it.get("grp_id"), it.get("bank")))

        pending = []
        POST_DELAY = 0

        def attn_pipeline_hook(items, hook):
            n = len(items)
            LAG = NS - 1
            for i in range(n + LAG):
                if i < n and i in hook:
                    hook[i]()
                for pp in list(pending):
                    if pp[0] <= i:
                        pending.remove(pp)
                        pp[1]()
                attn_step(items, i, n, LAG)
            for pp in list(pending):
                pending.remove(pp)
                pp[1]()

        def flat_view(nq):
            return lambda T: T[:, 0:nq]

        def pair_view(nq):
            return lambda T: T[:, 0:2 * nq].rearrange("p (b q) -> p b q", b=2)

        p2a = ExitStack()
        KAr = sb(p2a, "KAr", [128, 2, NT], BF16)
        VAd = sb(p2a, "VAd", [128, 32, 256], BF16)
        WAt = sb(p2a, "WAt", [128, 3, 8, 384], BF16)
        qtz = [sb(p2a, "qtz%d" % i, [128, 4, 2, 256], BF16) for i in range(2)]
        b_qtz = [Buf("qtz0"), Buf("qtz1")]
        b_kar, b_var, b_wat = Buf("KAr"), Buf("VAr"), Buf("WAt")
        for kvh in range(2):
            for hf in range(2):
                cx.dma(sp, None, KAr[64 * hf:64 * hf + 64, kvh, :], KA[64 * kvh:64 * kvh + 64, :], writes=[b_kar])
        cx.dma(sp, None, VAd[:], VA[:, :].rearrange("(c p) n -> p c n", p=128), writes=[b_var])
        cx.dma(sp, None, WAt[:], wa_d[:, :, :, :], writes=[b_wat])
        for i in range(2):
            cx.op(dve, lambda h: h.memset(qtz[i][0:64, :, 1, :], 0.0), writes=[b_qtz[i]])
            cx.op(dve, lambda h: h.memset(qtz[i][64:128, :, 0, :], 0.0), writes=[b_qtz[i]])
        QTA = 256

        def load_qz(qt):
            i = qt % 2
            q0 = qt * QTA
            cx.dma(sp, None, qtz[i][0:64, :, 0, :], QA[0:4, 0:64, q0:q0 + QTA].rearrange("c p t -> p c t"),
                   writes=[b_qtz[i]])
            cx.dma(sp, None, qtz[i][64:128, :, 1, :], QA[0:4, 64:128, q0:q0 + QTA].rearrange("c p t -> p c t"),
                   writes=[b_qtz[i]])

        items = []
        hook = {}
        grp = [0]

        def a_post(qt, c, gi):
            def f():
                nb, dbk = nbank[gi % 2], dbank[gi % 2]
                r_, br = rD[gi % 2], b_rD[gi % 2]
                o_, bo = ost[qt % 2], b_ost[qt % 2]
                cx.op(act, lambda h: h.activation(out=r_[:], in_=dbk[:, :], func=AF.Ln,
                                                  bias=par[:, P_SINKE + c:P_SINKE + c + 1]),
                      reads=[b_dbank[gi % 2], b_par], writes=[br])
                cx.op(act, lambda h: h.activation(out=r_[:], in_=r_[:], func=AF.Exp, scale=-1.0), reads=[br], writes=[br])
                cx.op(dve, lambda h: h.tensor_tensor(out=o_[0:64, c, 0:QTA], in0=nb[0:64, 0:QTA], in1=r_[0:64, 0:QTA],
                                                     op=ALU.mult),
                      reads=[b_nbank[gi % 2], br], writes=[bo])
                cx.op(dve, lambda h: h.tensor_tensor(out=o_[64:128, c, 0:QTA], in0=nb[64:128, QTA:2 * QTA],
                                                     in1=r_[64:128, QTA:2 * QTA], op=ALU.mult),
                      reads=[b_nbank[gi % 2], br], writes=[bo])
                if c == 3:
                    cx.dma(sp, None, OT[0:4, :, qt * QTA:(qt + 1) * QTA].rearrange("c p t -> p c t"), o_[:, :, 0:QTA],
                           reads=[bo])
            return f

        load_qz(0)
        for qt in range(NT // QTA):
            q0 = qt * QTA
            if qt + 1 < NT // QTA:
                hook[len(items)] = (lambda q: (lambda: load_qz(q)))(qt + 1)
            for c in range(4):
                kvh = c // 2
                gi = grp[0]
                grp[0] += 1
                kcs = list(range(max(0, q0 // 128 - 1), min(32, q0 // 128 + 3)))
                nv = nbank[gi % 2][:, :].rearrange("p (b q) -> p b q", b=2)
                dv = dbank[gi % 2][:, :].rearrange("p (b q) -> p b q", b=2)
                for kc in kcs:
                    qs, qe = max(q0, 128 * kc - 128), min(q0 + QTA, 128 * kc + 256)
                    nq = qe - qs
                    var = 1 if kc == 15 else (2 if kc == 16 else 0)
                    b0 = qs - (128 * kc - 128)
                    items.append(dict(
                        nq=2 * nq, s_view=pair_view(nq), k=KAr[:, kvh, 128 * kc:128 * kc + 128],
                        q=qtz[qt % 2][:, c, :, qs - q0:qe - q0], kq_bufs=[b_kar, b_qtz[qt % 2]],
                        scale=0.125, shift=-SHIFT_A, mask=WAt[:, var, 2 * c:2 * c + 2, b0:b0 + nq], mask_bufs=[b_wat],
                        pv=([(nv[:, :, qs - q0:qe - q0], VAd[:, kc, kvh * 128:kvh * 128 + 128]),
                             (dv[:, :, qs - q0:qe - q0], ones128[:])] if nq == QTA else
                            [(nv[:, 0, qs - q0:qe - q0], VAd[:, kc, kvh * 128:kvh * 128 + 128], 0, nq),
                             (dv[:, 0, qs - q0:qe - q0], ones128[:], 0, nq),
                             (nv[:, 1, qs - q0:qe - q0], VAd[:, kc, kvh * 128:kvh * 128 + 128], nq, nq),
                             (dv[:, 1, qs - q0:qe - q0], ones128[:], nq, nq)]),
                        start=(kc == kcs[0]), stop=(kc == kcs[-1]), v_bufs=[b_var, b_ones],
                        out_bufs=[b_nbank[gi % 2], b_dbank[gi % 2]], grp_id=gi, bank=gi % 2))
                items[-1]["post"] = a_post(qt, c, gi)
        attn_pipeline_hook(items, hook)
        cx.barrier()
        p2a.close()

        p2b = ExitStack()
        WBt = sb(p2b, "WBt", [128, 3, 12, 256], BF16)
        b_wbt = Buf("WBt")
        cx.dma(sp, None, WBt[:], wb_d[:, :, :, :], writes=[b_wbt])
        Nacc = sb(p2b, "Nacc", [128, 4, 2048], F32)
        Dacc = sb(p2b, "Dacc", [128, 4, 2048], F32)
        b_nacc = [Buf("Nacc%d" % i) for i in range(4)]
        b_dacc = [Buf("Dacc%d" % i) for i in range(4)]
        RB = 4
        ksl = [sb(p2b, "ksl%d" % i, [128, 4, 512], BF16) for i in range(RB)]
        vsl = [sb(p2b, "vsl%d" % i, [128, 4, 512], BF16) for i in range(RB)]
        qtb = [sb(p2b, "qtb%d" % i, [128, 4, 256], BF16) for i in range(RB)]
        b_ksl = [Buf("ksl%d" % i) for i in range(RB)]
        b_vsl = [Buf("vsl%d" % i) for i in range(RB)]
        b_qtb = [Buf("qtb%d" % i) for i in range(RB)]
        def load_q(src, qt, nq=4):
            i = qt % 2
            cx.dma(sp, None, qt_[i][:], src[0:nq, :, qt * TT:(qt + 1) * TT].rearrange("c p t -> p c t"),
                   writes=[b_qt[i]])

        items = []
        hook = {}
        load_q(QM, 0)
        sc_m = float(128 ** -0.5)

        def m_post(qt, hh, gi):
            def f():
                nb, dbk = nbank[gi % 2], dbank[gi % 2]
                r_, br = rD[gi % 2], b_rD[gi % 2]
                o_, bo = ost[qt % 2], b_ost[qt % 2]
                cx.op(act, lambda h: h.activation(out=r_[:], in_=dbk[:, :], func=AF.Ln),
                      reads=[b_dbank[gi % 2]], writes=[br])
                cx.op(act, lambda h: h.activation(out=r_[:], in_=r_[:], func=AF.Exp, scale=-1.0), reads=[br], writes=[br])
                cx.op(dve, lambda h: h.tensor_tensor(out=o_[:, hh, :], in0=nb[:, :], in1=r_[:], op=ALU.mult),
                      reads=[b_nbank[gi % 2], br], writes=[bo])
                if hh == 3:
                    cx.dma(sp, sl_ost[qt % 2], OT[8:12, :, qt * TT:(qt + 1) * TT].rearrange("c p t -> p c t"), o_[:],
                           reads=[bo])
            return f

        for qt in range(NTT):
            st_ = qt // 4
            if qt + 1 < NTT:
                hook[len(items)] = (lambda q: (lambda: load_q(QM, q)))(qt + 1)
            for hh in range(4):
                gi = grp[0]
                grp[0] += 1
                for j in range(2):
                    items.append(dict(
                        nq=TT, s_view=flat_view(TT), k=MKT[:, hh, 256 * st_ + 128 * j:256 * st_ + 128 * j + 128], q=qt_[qt % 2][:, hh, :],
                        kq_bufs=[b_mkt, b_qt[qt % 2]], scale=sc_m, shift=-SHIFT_B, mask=None, mask_bufs=[],
                        pv=[(nbank[gi % 2][:, :], MV[:, 2 * st_ + j, hh * 128:hh * 128 + 128]),
                            (dbank[gi % 2][:, :], ones128[:])],
                        start=(j == 0), stop=(j == 1), v_bufs=[b_mv, b_ones],
                        out_bufs=[b_nbank[gi % 2], b_dbank[gi % 2]], qt=qt, grp_id=gi, bank=gi % 2))
                items[-1]["post"] = m_post(qt, hh, gi)
        attn_pipeline_hook(items, hook)

        sc_b = float(128 ** -0.5)
        tile_ctr = [0]

        def hb_view(HB, nq):
            return lambda T: T[:, 0:HB * nq].rearrange("p (b q) -> p b q", b=HB)

        for S2 in range(2):
            items = []
            hook = {}
            for g in range(3):
                d = B_DIL[g]
                Ls = NT // d
                half = Ls // 2
                nqt = min(256, half)
                HB = 512 // nqt if nqt < 256 else 2
                HB = min(HB, 4)
                for r in range(d):
                    for p0 in range(S2 * half, (S2 + 1) * half, nqt):
                        ti = tile_ctr[0]
                        tile_ctr[0] += 1
                        kc_lo = max(0, (p0 - 64) // 128)
                        kc_hi = min(Ls // 128 - 1, (p0 + nqt - 1 + 64) // 128)
                        nk = kc_hi - kc_lo + 1
                        base = r * Ls

                        def loader(ti=ti, g=g, d=d, r=r, p0=p0, nqt=nqt, kc_lo=kc_lo, nk=nk, base=base):
                            i2 = ti % RB
                            cx.dma(sp, None, qtb[i2][:, :, 0:nqt],
                                   QB[4 * g:4 * g + 4, :, base + p0:base + p0 + nqt].rearrange("c p t -> p c t"),
                                   writes=[b_qtb[i2]])
                            cx.dma(sp, None, ksl[i2][:, :, 0:nk * 128],
                                   KB[4 * g:4 * g + 4, :, base + 128 * kc_lo:base + 128 * (kc_lo + nk)].rearrange(
                                       "c p t -> p c t"), writes=[b_ksl[i2]])
                            row0 = 128 * kc_lo * d + r
                            nrow = nk * 128
                            if d == 1:
                                src = VB[g, row0:row0 + nrow, :]
                            else:
                                src = VB[g, row0:row0 + (nrow - 1) * d + 1:d, :]
                            cx.dma(sp, None, vsl[i2][:, 0:nk, :], src.rearrange("(kc a) c -> a kc c", a=128),
                                   writes=[b_vsl[i2]])
                        hook.setdefault(len(items), []).append(loader)
                        i2 = ti % RB
                        nat0 = (p0 - S2 * half) * d + r
                        for h0 in range(0, 4, HB):
                            gi = grp[0]
                            grp[0] += 1
                            nv = nbank[gi % 2][:, 0:HB * nqt].rearrange("p (b q) -> p b q", b=HB)
                            dv = dbank[gi % 2][:, 0:HB * nqt].rearrange("p (b q) -> p b q", b=HB)
                            its = []
                            for kc in range(kc_lo, kc_hi + 1):
                                qs, qe = max(p0, 128 * kc - 64), min(p0 + nqt, 128 * kc + 192)
                                nq = qe - qs
                                if nq <= 0:
                                    continue
                                var = 1 if kc == Ls // 256 - 1 else (2 if kc == Ls // 256 else 0)
                                b0_ = qs - (128 * kc - 64)
                                s_list = []
                                pvl = []
                                for hb in range(HB):
                                    hh = h0 + hb
                                    s_list.append((ksl[i2][:, hh, (kc - kc_lo) * 128:(kc - kc_lo) * 128 + 128],
                                                   qtb[i2][:, hh, qs - p0:qe - p0], hb * nq, nq))
                                    pvl.append((nv[:, hb, qs - p0:qe - p0], vsl[i2][:, kc - kc_lo, hh * 128:hh * 128 + 128],
                                                hb * nq, nq))
                                    pvl.append((dv[:, hb, qs - p0:qe - p0], ones128[:], hb * nq, nq))
                                its.append(dict(
                                    nq=HB * nq, s_view=hb_view(HB, nq), s_list=s_list,
                                    kq_bufs=[b_ksl[i2], b_qtb[i2]],
                                    scale=sc_b, shift=-SHIFT_B,
                                    mask=WBt[:, var, 4 * g + h0:4 * g + h0 + HB, b0_:b0_ + nq], mask_bufs=[b_wbt],
                                    pv=pvl, start=False, stop=False, v_bufs=[b_vsl[i2], b_ones], pool_share=True,
                                    out_bufs=[b_nbank[gi % 2], b_dbank[gi % 2]], grp_id=gi, bank=gi % 2))
                            its[0]["start"] = True
                            its[-1]["stop"] = True

                            def b_post(gi=gi, h0=h0, HB=HB, g=g, d=d, nat0=nat0, nqt=nqt):
                                nv_ = nbank[gi % 2][:, 0:HB * nqt].rearrange("p (b q) -> p b q", b=HB)
                                dv_ = dbank[gi % 2][:, 0:HB * nqt].rearrange("p (b q) -> p b q", b=HB)
                                if d == 1:
                                    na = Nacc[:, h0:h0 + HB, nat0:nat0 + nqt]
                                    da = Dacc[:, h0:h0 + HB, nat0:nat0 + nqt]
                                else:
                                    na = Nacc[:, h0:h0 + HB, nat0:nat0 + (nqt - 1) * d + 1:d]
                                    da = Dacc[:, h0:h0 + HB, nat0:nat0 + (nqt - 1) * d + 1:d]
                                bn_ = [b_nacc[h0 + k_] for k_ in range(HB)]
                                bd_ = [b_dacc[h0 + k_] for k_ in range(HB)]
                                if g == 0:
                                    cx.op(act, lambda h: h.activation(out=na, in_=nv_, func=AF.Copy),
                                          reads=[b_nbank[gi % 2]], writes=bn_)
                                    cx.op(act, lambda h: h.activation(out=da, in_=dv_, func=AF.Copy),
                                          reads=[b_dbank[gi % 2]], writes=bd_)
                                else:
                                    cx.op(dve, lambda h: h.tensor_tensor(out=na, in0=nv_, in1=na, op=ALU.add),
                                          reads=[b_nbank[gi % 2]] + bn_, writes=bn_)
                                    r_, br = rD[gi % 2], b_rD[gi % 2]
                                    rv = r_[:, 0:HB * nqt].rearrange("p (b q) -> p b q", b=HB)
                                    cx.op(act, lambda h: h.activation(out=rv, in_=dv_, func=AF.Copy),
                                          reads=[b_dbank[gi % 2]], writes=[br])
                                    cx.op(pool, lambda h: h.tensor_tensor(out=da, in0=rv, in1=da, op=ALU.add),
                                          reads=[br] + bd_, writes=bd_)
                            its[-1]["post"] = b_post
                            items.extend(its)
            starts = sorted(hook)
            hook2 = {}
            for si, st in enumerate(starts):
                at = (starts[si - RB + 1] + NS - 1) if si >= RB else -1
                assert at < st
                hook2.setdefault(at, []).extend(hook[st])
            for f in hook2.get(-1, []):
                f()
            hk = {k_: (lambda fs: (lambda: [f() for f in fs]))(v) for k_, v in hook2.items() if k_ >= 0}
            attn_pipeline_hook(items, hk)
            for hh in range(4):
                cx.op(act, lambda h: h.activation(out=Dacc[:, hh, :], in_=Dacc[:, hh, :], func=AF.Ln),
                      reads=[b_dacc[hh]], writes=[b_dacc[hh]])
                cx.op(act, lambda h: h.activation(out=Dacc[:, hh, :], in_=Dacc[:, hh, :], func=AF.Exp, scale=-1.0),
                      reads=[b_dacc[hh]], writes=[b_dacc[hh]])
            for q4 in range(4):
                o_, bo = ost[q4 % 2], b_ost[q4 % 2]
                for hh in range(4):
                    cx.op(pool, lambda h: h.tensor_tensor(out=o_[:, hh, :], in0=Nacc[:, hh, q4 * 512:(q4 + 1) * 512],
                                                          in1=Dacc[:, hh, q4 * 512:(q4 + 1) * 512], op=ALU.mult),
                          reads=[b_nacc[hh], b_dacc[hh]], writes=[bo])
                t0 = S2 * 2048 + q4 * 512
                cx.dma(sp, sl_ost[q4 % 2], OT[4:8, :, t0:t0 + 512].rearrange("c p t -> p c t"), o_[:], reads=[bo])
        cx.barrier()
        p2b.close()
        p2.close()
        if debug == 2:
            pm.close()
            pw3.close()
            return nc
        pm.close()

        p3 = ExitStack()
        oTt = [sb(p3, "oTt%d" % i, [128, 12, TT], BF16) for i in range(2)]
        b_oTt = [Buf("oTt0"), Buf("oTt1")]
        sl_oTt = [cx.slot(), cx.slot()]
        gTt = [sb(p3, "gTt%d" % i, [128, 24, TT], BF16) for i in range(2)]
        b_gTt = [Buf("gTt0"), Buf("gTt1")]
        xt3 = [sb(p3, "xt3_%d" % i, [128, 4, D], F32) for i in range(2)]
        b_xt3 = [Buf("xt3_0"), Buf("xt3_1")]
        sl_xt3 = [cx.slot(), cx.slot()]
        sl_y3 = [cx.slot(), cx.slot()]
        zT = [sb(p3, "zT%d" % i, [128, 8, TT], BF16) for i in range(2)]
        b_zT = [Buf("zT0"), Buf("zT1")]
        tb = [sb(p3, "tb%d" % i, [128, TT], F32) for i in range(6)]
        b_tb = [Buf("tb%d" % i) for i in range(6)]
        h2 = [sb(p3, "h2_%d" % i, [128, D], BF16) for i in range(2)]
        b_h2 = [Buf("h2_0"), Buf("h2_1")]
        h2Ts = [sb(p3, "h2Ts%d" % i, [128, 8, TT], BF16) for i in range(2)]
        b_h2Ts = [Buf("h2Ts0"), Buf("h2Ts1")]
        sl_h2Ts = [cx.slot(), cx.slot()]
        st3 = sb(p3, "st3", [128, 3, 32], F32)
        b_st3 = [Buf("st3_%d" % j) for j in range(32)]
        sqj3 = sb(p3, "sqj3", [128, D], BF16)
        b_sqj3 = Buf("sqj3")
        ybr = [ps(p3, "ybr%d" % i, [128, 512], F32) for i in range(4)]
        b_ybr = [Buf("ybr%d" % i) for i in range(4)]
        acc3 = [ps(p3, "acc3_%d" % i, [128, 512], F32) for i in range(2)]
        b_acc3 = [Buf("acc3_0"), Buf("acc3_1")]
        pT3 = [ps(p3, "pT3_%d" % i, [128, 8, 128], BF16) for i in range(2)]
        b_pT3 = [Buf("pT3_0"), Buf("pT3_1")]

        def p3_og_loads(t):
            i = t % 2
            cx.dma(sp, None, oTt[i][:], OT[:, :, t * TT:(t + 1) * TT].rearrange("c p t -> p c t"),
                   writes=[b_oTt[i]])
            cx.dma(sp, None, gTt[i][:], GT[:, :, t * TT:(t + 1) * TT].rearrange("c p t -> p c t"), writes=[b_gTt[i]])

        def p3_x_loads(t):
            i = t % 2
            cx.dma(sp, None, xt3[i][:], x_d[t * TT:(t + 1) * TT, :].rearrange("(s p) d -> p s d", p=128),
                   writes=[b_xt3[i]])

        yc = [0]

        def merge_step(t, dc):
            i = t % 2
            o_, bo = oTt[i], b_oTt[i]
            g_, bg = gTt[i], b_gTt[i]
            z_, bz = zT[i], b_zT[i]
            tset = (dc % 2) * 3
            for b in range(3):
                Y, bY = ybr[yc[0] % 4], b_ybr[yc[0] % 4]
                yc[0] += 1

                def mm(h):
                    for j in range(4):
                        ins = h.matmul(Y[:, :], lhsT=wbr[:, b * 4 + j, dc * 128:(dc + 1) * 128],
                                       rhs=o_[:, b * 4 + j, :], start=(j == 0), stop=(j == 3))
                    return ins
                cx.op(pe, mm, reads=[b_wbr, bo], writes=[bY])
                cx.op(dve, lambda h: h.tensor_tensor(out=tb[tset + b][:], in0=Y[:, :], in1=g_[:, b * 8 + dc, :],
                                                     op=ALU.mult),
                      reads=[bY, bg], writes=[b_tb[tset + b]])
            cx.op(pool, lambda h: h.tensor_tensor(out=tb[tset][:], in0=tb[tset][:], in1=tb[tset + 1][:], op=ALU.add),
                  reads=[b_tb[tset], b_tb[tset + 1]], writes=[b_tb[tset]])
            cx.op(pool, lambda h: h.tensor_tensor(out=z_[:, dc, :], in0=tb[tset][:], in1=tb[tset + 2][:], op=ALU.add),
                  reads=[b_tb[tset], b_tb[tset + 2]], writes=[bz])

        def wa_step(t, s_):
            i = t % 2
            x_, bx = xt3[i], b_xt3[i]
            z_, bz = zT[i], b_zT[i]
            j = t * 4 + s_
            for half in range(2):
                A, bA = acc3[half], b_acc3[half]

                def mm2(h):
                    for kc in range(8):
                        ins = h.matmul(A[:, :], lhsT=z_[:, kc, s_ * 128:(s_ + 1) * 128],
                                       rhs=wout[:, kc, half * 512:(half + 1) * 512], start=(kc == 0), stop=(kc == 7))
                    return ins
                cx.op(pe, mm2, reads=[bz, b_wout], writes=[bA])
                cx.op(dve, lambda h: h.tensor_tensor(out=x_[:, s_, half * 512:(half + 1) * 512], in0=A[:, :],
                                                     in1=x_[:, s_, half * 512:(half + 1) * 512], op=ALU.add),
                      reads=[bA, bx], writes=[bx])
            cx.op(act, lambda h: h.activation(out=sqj3[:], in_=x_[:, s_, :], func=AF.Square,
                                              accum_out=st3[:, 0, j:j + 1]), reads=[bx], writes=[b_sqj3, b_st3[j]])
            cx.op(act, lambda h: h.activation(out=st3[:, 1, j:j + 1], in_=st3[:, 0, j:j + 1], func=AF.Ln,
                                              scale=1.0 / D, bias=EPS), reads=[b_st3[j]], writes=[b_st3[j]])
            cx.op(act, lambda h: h.activation(out=st3[:, 2, j:j + 1], in_=st3[:, 1, j:j + 1], func=AF.Exp,
                                              scale=-0.5), reads=[b_st3[j]], writes=[b_st3[j]])
            hb_, bhb = h2[j % 2], b_h2[j % 2]
            cx.op(act, lambda h: h.activation(out=hb_[:], in_=x_[:, s_, :], func=AF.Copy,
                                              scale=st3[:, 2, j:j + 1]), reads=[bx, b_st3[j]], writes=[bhb])

        def wb_step(t, s_):
            i = t % 2
            j = t * 4 + s_
            hs, bhs = h2Ts[i], b_h2Ts[i]
            hb_, bhb = h2[j % 2], b_h2[j % 2]
            pt_, bpt = pT3[j % 2], b_pT3[j % 2]

            def tr3(h):
                for kc in range(8):
                    ins = h.transpose(out=pt_[:, kc, :], in_=hb_[:, kc * 128:(kc + 1) * 128], identity=ident[:])
                return ins
            cx.op(pe, tr3, reads=[bhb, b_ident], writes=[bpt])
            cx.op(dve, lambda h: h.tensor_tensor(out=hs[:, :, s_ * 128:(s_ + 1) * 128], in0=pt_[:],
                                                 in1=par[:, P_GMLP:P_GMLP + 8].unsqueeze(2).to_broadcast([128, 8, 128]),
                                                 op=ALU.mult), reads=[bpt, b_par], writes=[bhs])

        def p3_stores(t):
            i = t % 2
            cx.dma(sp, None, y_d[t * TT:(t + 1) * TT, :].rearrange("(s p) d -> p s d", p=128), xt3[i][:],
                   reads=[b_xt3[i]])
            cx.dma(sp, None, H2T[:, :, t * TT:(t + 1) * TT].rearrange("c p t -> p c t"), h2Ts[i][:],
                   reads=[b_h2Ts[i]])

        p3_og_loads(0)
        p3_x_loads(0)
        p3_og_loads(1)
        for dc in range(8):
            merge_step(0, dc)
        msched = [(0, 1, 2), (3, 4, 5), (6, 7), ()]
        for t in range(NTT):
            nxt = t + 1 < NTT
            if nxt:
                p3_x_loads(t + 1)
            if t + 2 < NTT:
                p3_og_loads(t + 2)
            for s_ in range(4):
                wa_step(t, s_)
                if s_ >= 1:
                    wb_step(t, s_ - 1)
                if nxt:
                    for dc in msched[s_]:
                        merge_step(t + 1, dc)
            wb_step(t, 3)
            p3_stores(t)
        cx.barrier()
        p3.close()
        pw3.close()

        p4 = ExitStack()
        wup = sb(p4, "wup", [128, 8, 4096], BF16)
        wdn = sb(p4, "wdn", [128, 32, D], BF16)
        b_wup = [Buf("wup%d" % i) for i in range(4)]
        b_wdn = [Buf("wdn%d" % i) for i in range(4)]
        sl_w4 = cx.slot()
        for i in range(4):
            cx.dma(pool, sl_w4, wup[:, :, 1024 * i:1024 * (i + 1)],
                   w_up_d[:, 1024 * i:1024 * (i + 1)].rearrange("(kc p) n -> p kc n", p=128), writes=[b_wup[i]])
        for i in range(4):
            cx.dma(pool, sl_w4, wdn[:, 8 * i:8 * (i + 1), :],
                   w_dn_d[1024 * i:1024 * (i + 1), :].rearrange("(c p) n -> p c n", p=128), writes=[b_wdn[i]])
        h2Tt = [sb(p4, "h2Tt%d" % i, [128, 8, TT], BF16) for i in range(2)]
        b_h2Tt = [Buf("h2Tt0"), Buf("h2Tt1")]
        sl_h2Tt = [cx.slot(), cx.slot()]
        x4 = sb(p4, "x4", [128, 4, D], F32)
        b_x4 = Buf("x4")
        sl_x4 = cx.slot()
        sl_y4 = cx.slot()
        upT = sb(p4, "upT", [128, 32, TT], BF16)
        b_upT = [Buf("upT%d" % c) for c in range(32)]
        rl = [sb(p4, "rl%d" % i, [128, TT], F32) for i in range(2)]
        b_rl = [Buf("rl0"), Buf("rl1")]
        upp = [ps(p4, "upp%d" % i, [128, 512], F32) for i in range(3)]
        b_upp = [Buf("upp%d" % i) for i in range(3)]
        acc4 = [ps(p4, "acc4_%d" % i, [128, 512], F32) for i in range(2)]
        b_acc4 = [Buf("acc4_0"), Buf("acc4_1")]

        def p4_hload(t):
            i = t % 2
            cx.dma(sp, sl_h2Tt[i], h2Tt[i][:], H2T[:, :, t * TT:(t + 1) * TT].rearrange("c p t -> p c t"),
                   writes=[b_h2Tt[i]])

        def p4_xload(t):
            cx.dma(sp, sl_x4, x4[:], y_d[t * TT:(t + 1) * TT, :].rearrange("(s p) d -> p s d", p=128), writes=[b_x4])

        p4_hload(0)
        p4_xload(0)
        uc = 0
        for t in range(NTT):
            if t + 1 < NTT:
                p4_hload(t + 1)
            hT_, bh = h2Tt[t % 2], b_h2Tt[t % 2]
            for c in range(32):
                U, bU = upp[uc % 3], b_upp[uc % 3]
                r_, br = rl[uc % 2], b_rl[uc % 2]
                uc += 1

                def mmu(h):
                    for kc in range(8):
                        ins = h.matmul(U[:, :], lhsT=wup[:, kc, c * 128:(c + 1) * 128], rhs=hT_[:, kc, :],
                                       start=(kc == 0), stop=(kc == 7))
                    return ins
                cx.op(pe, mmu, reads=[b_wup[c // 8], bh], writes=[bU])
                cx.op(act, lambda h: h.activation(out=r_[:], in_=U[:, :], func=AF.Relu), reads=[bU], writes=[br])
                cx.op(dve, lambda h: h.tensor_tensor(out=upT[:, c, :], in0=r_[:], in1=r_[:], op=ALU.mult),
                      reads=[br], writes=[b_upT[c]])
            for s in range(4):
                for half in range(2):
                    A, bA = acc4[half], b_acc4[half]

                    def mmd(h):
                        for c in range(32):
                            ins = h.matmul(A[:, :], lhsT=upT[:, c, s * 128:(s + 1) * 128],
                                           rhs=wdn[:, c, half * 512:(half + 1) * 512], start=(c == 0), stop=(c == 31))
                        return ins
                    cx.op(pe, mmd, reads=b_upT + b_wdn, writes=[bA])
                    cx.op(dve, lambda h: h.tensor_tensor(out=x4[:, s, half * 512:(half + 1) * 512], in0=A[:, :],
                                                         in1=x4[:, s, half * 512:(half + 1) * 512], op=ALU.add),
                          reads=[bA, b_x4], writes=[b_x4])
            cx.dma(sp, sl_y4, y_d[t * TT:(t + 1) * TT, :].rearrange("(s p) d -> p s d", p=128), x4[:], reads=[b_x4])
            if t + 1 < NTT:
                p4_xload(t + 1)
        cx.barrier()
        p4.close()
    return nc


def alibi(n):
    return (2.0 ** (-8.0 * (np.arange(n) + 1) / n)).astype(np.float64)


def make_tables(is_prompt):
    sa = alibi(8)
    a = np.arange(128)[:, None]
    b = np.arange(384)[None, :]
    diff = b - 128 - a
    wa = np.zeros((128, 3, 8, 384), np.float64)
    for h in range(8):
        base = np.where(np.abs(diff) <= 128, np.exp(-sa[h] * np.abs(diff)), 0.0)
        wa[:, 0, h] = base
        lo = base.copy()
        hi = base.copy()
        if is_prompt:
            lo[:, 256:] = 0.0
            hi[:, :128] = 0.0
        wa[:, 1, h] = lo
        wa[:, 2, h] = hi
    sb_ = alibi(12)
    b2 = np.arange(256)[None, :]
    diff2 = b2 - 64 - a
    wb = np.zeros((128, 3, 12, 256), np.float64)
    for g in range(3):
        for hh in range(4):
            j = g * 4 + hh
            base = np.where(np.abs(diff2) <= 64, np.exp(-sb_[j] * np.abs(diff2) * B_DIL[g]), 0.0)
            wb[:, 0, j] = base
            lo = base.copy()
            hi = base.copy()
            if is_prompt:
                lo[:, 192:] = 0.0
                hi[:, :64] = 0.0
            wb[:, 1, j] = lo
            wb[:, 2, j] = hi
    return wa.astype(NPBF), wb.astype(NPBF)


def make_params(g_mix, g_mem, g_mlp, b_gate, gq_a, gk_a, gq_b, gk_b, gq_m, gk_m, sink_a):
    p = np.zeros((128, P_N), np.float32)
    p[:, P_GMIX:P_GMIX + 8] = g_mix.reshape(8, 128).T
    p[:, P_GMEM:P_GMEM + 8] = g_mem.reshape(8, 128).T
    p[:, P_GMLP:P_GMLP + 8] = g_mlp.reshape(8, 128).T
    p[:, P_BG:P_BG + 24] = b_gate.reshape(24, 128).T
    p[:, P_GQA] = np.concatenate([gq_a.reshape(64)] * 2)
    p[:, P_GKA] = np.concatenate([gk_a.reshape(64)] * 2)
    p[:, P_GQB] = gq_b.reshape(128)
    p[:, P_GKB] = gk_b.reshape(128)
    p[:, P_GQM] = gq_m.reshape(128)
    p[:, P_GKM] = gk_m.reshape(128)
    s = sink_a.reshape(8)
    for c in range(4):
        p[0:64, P_SINK + c] = s[2 * c]
        p[64:128, P_SINK + c] = s[2 * c + 1]
    return p


def make_in_maps(x_prompt, x_sample, mem_prompt, mem_sample, g_mix, g_mem, w_in, b_gate, w_mem_kv,
                 gq_a, gk_a, sink_a, gq_b, gk_b, gq_m, gk_m, w_branch, w_out, g_mlp, w_up, w_down):
    f = lambda a: np.ascontiguousarray(np.asarray(a, dtype=np.float32))
    params = make_params(f(g_mix), f(g_mem), f(g_mlp), f(b_gate), f(gq_a), f(gk_a), f(gq_b), f(gk_b), f(gq_m),
                         f(gk_m), f(sink_a))
    ident = np.eye(128, dtype=np.float32).astype(NPBF)
    tabs = {True: make_tables(True), False: make_tables(False)}
    shared = dict(w_in=f(w_in).reshape(D, 8960), w_mem_kv=f(w_mem_kv).reshape(D, 1024),
                  w_branch=f(w_branch).reshape(1536, D), w_out=f(w_out).reshape(D, D),
                  w_up=f(w_up).reshape(D, 4096), w_down=f(w_down).reshape(4096, D),
                  params=params, ident=ident)
    xp, xs, mp, ms = f(x_prompt), f(x_sample), f(mem_prompt), f(mem_sample)
    in_maps = []
    for c in range(8):
        m = dict(shared)
        if c < 4:
            m["x"] = xp[2 * c:2 * c + 2].reshape(NT, D)
            m["mem"] = mp[2 * c:2 * c + 2].reshape(NMEM, D)
            wa, wb = tabs[True]
        else:
            m["x"] = xs[c - 4].reshape(NT, D)
            m["mem"] = np.concatenate([ms[c - 4], ms[c - 4]], axis=0)
            wa, wb = tabs[False]
        m["wa_tab"], m["wb_tab"] = wa, wb
        in_maps.append(m)
    return in_maps


_NC_CACHE = {}


def kernel(**inputs):
    in_maps = make_in_maps(**inputs)
    if "nc" not in _NC_CACHE:
        _NC_CACHE["nc"] = build_program()
    nc = _NC_CACHE["nc"]
    res = run_bass_kernel_spmd(nc, in_maps, core_ids=list(range(8)))
    ys = [np.asarray(r["y"], dtype=np.float32) for r in res.results]
    y_prompt = np.stack([ys[c].reshape(2, 2048, D) for c in range(4)], 0).reshape(8, 2048, D)
    y_sample = np.stack([ys[c] for c in range(4, 8)], 0)
    return (y_prompt, y_sample)
```

```python
import numpy as np
import ml_dtypes
from contextlib import ExitStack
import concourse.bass as bass
import concourse.mybir as mybir
from concourse.bass_utils import run_bass_kernel_spmd

F32 = mybir.dt.float32
BF16 = mybir.dt.bfloat16
AF = mybir.ActivationFunctionType
ALU = mybir.AluOpType
NPBF = ml_dtypes.bfloat16

NT = 4096
D = 1024
NMEM = 512
EPS = 1e-6
TT = 512
NTT = NT // TT
B_DIL = (1, 4, 16)

C_QA, C_KA, C_VA, C_QB, C_KB, C_VB, C_QM, C_GT = 0, 512, 640, 768, 2304, 3840, 5376, 5888

P_GMIX, P_GMEM, P_GMLP, P_BG, P_GQA, P_GKA, P_GQB, P_GKB, P_GQM, P_GKM, P_SINK, P_NEGB, P_SINKE, P_N = \
    0, 8, 16, 24, 48, 49, 50, 51, 52, 53, 54, 58, 82, 86

SHIFT_A = 8.0
SHIFT_B = 11.5
SAME_ENG_SYNC = True


class Buf:
    __slots__ = ("name", "w", "r", "slot")

    def __init__(self, name):
        self.name = name
        self.w = None
        self.r = {}
        self.slot = None


class Sem:
    _n = 0

    def __init__(self, h):
        self.h = h
        Sem._n += 1
        self.key = Sem._n


class Eng:
    def __init__(self, name, h, sem):
        self.name, self.h, self.sem = name, h, sem
        self.count = 0
        self.waited = {}


class Slot:
    def __init__(self, sem):
        self.sem = sem
        self.count = 0


class Ctx:
    def __init__(self, nc, es):
        self.nc = nc
        self.es = es
        mk = lambda n: Sem(es.enter_context(nc.semaphore(n)))
        self.pe = Eng("pe", nc.tensor, mk("s_pe"))
        self.act = Eng("act", nc.scalar, mk("s_act"))
        self.dve = Eng("dve", nc.vector, mk("s_dve"))
        self.pool = Eng("pool", nc.gpsimd, mk("s_pool"))
        self.sp = Eng("sp", nc.sync, mk("s_sp"))
        self.engs = [self.pe, self.act, self.dve, self.pool, self.sp]
        self.slots = []
        self.nslot = 0

    def slot(self):
        return None

    def _buf_slot(self, b):
        if b.slot is None:
            self.nslot += 1
            b.slot = Slot(Sem(self.es.enter_context(self.nc.semaphore("s_dma%d" % self.nslot))))
            self.slots.append(b.slot)
        return b.slot

    def _wait(self, e, tok):
        sem, val, owner = tok
        if owner is e and (e.name == "pe" or not SAME_ENG_SYNC):
            return
        if e.waited.get(sem.key, 0) >= val:
            return
        e.h.wait_ge(sem.h, val)
        e.waited[sem.key] = val

    def _deps(self, e, reads, writes):
        toks = []
        for b in reads:
            if b.w is not None:
                toks.append(b.w)
        for b in writes:
            if b.w is not None:
                toks.append(b.w)
            toks.extend(b.r.values())
        for t in toks:
            self._wait(e, t)

    def _commit(self, tok, reads, writes):
        for b in reads:
            old = b.r.get(tok[0].key)
            if old is None or old[1] < tok[1]:
                b.r[tok[0].key] = tok
        for b in writes:
            b.w = tok
            b.r = {}

    def op(self, e, fn, reads=(), writes=()):
        self._deps(e, reads, writes)
        inst = fn(e.h)
        e.count += 1
        inst.then_inc(e.sem.h, 1)
        tok = (e.sem, e.count, e)
        self._commit(tok, reads, writes)
        return tok

    def dma(self, e, slot, out, in_, reads=(), writes=()):
        slot = self._buf_slot((list(writes) + list(reads))[0])
        self._deps(e, reads, writes)
        inst = e.h.dma_start(out=out, in_=in_)
        slot.count += 16
        inst.then_inc(slot.sem.h, 16)
        tok = (slot.sem, slot.count, None)
        self._commit(tok, reads, writes)
        return tok

    def barrier(self):
        for e in self.engs:
            for f in self.engs:
                if f is not e and f.count > 0:
                    self._wait(e, (f.sem, f.count, f))
            for s in self.slots:
                if s.count > 0:
                    self._wait(e, (s.sem, s.count, None))


def build_program(debug=False):
    nc = bass.Bass("TRN2", target_bir_lowering=False)
    dt_in = lambda n, s, d=F32: nc.dram_tensor(n, s, d, kind="ExternalInput").ap()
    skind = "ExternalOutput" if debug else "Internal"
    dt_s = lambda n, s, d=BF16: nc.dram_tensor(n, s, d, kind=skind).ap()

    x_d = dt_in("x", [NT, D])
    mem_d = dt_in("mem", [NMEM, D])
    w_in_d = dt_in("w_in", [D, 8960])
    w_mkv_d = dt_in("w_mem_kv", [D, 1024])
    w_br_d = dt_in("w_branch", [1536, D])
    w_out_d = dt_in("w_out", [D, D])
    w_up_d = dt_in("w_up", [D, 4096])
    w_dn_d = dt_in("w_down", [4096, D])
    par_d = dt_in("params", [128, P_N])
    ident_d = dt_in("ident", [128, 128], BF16)
    wa_d = dt_in("wa_tab", [128, 3, 8, 384], BF16)
    wb_d = dt_in("wb_tab", [128, 3, 12, 256], BF16)
    y_d = nc.dram_tensor("y", [NT, D], F32, kind="ExternalOutput").ap()

    QA = dt_s("sQA", [4, 128, NT])
    KA = dt_s("sKA", [128, NT])
    QB = dt_s("sQB", [12, 128, NT])
    KB = dt_s("sKB", [12, 128, NT])
    QM = dt_s("sQM", [4, 128, NT])
    GT = dt_s("sGT", [24, 128, NT])
    VA = dt_s("sVA", [NT, 256])
    VB = dt_s("sVB", [3, NT, 512])
    OT = dt_s("sOT", [12, 128, NT])
    H2T = dt_s("sH2T", [8, 128, NT])

    with ExitStack() as es:
        es.enter_context(nc.allow_low_precision(reason="bf16 matmul operands, fp32 accumulation"))
        cx = Ctx(nc, es)
        pe, act, dve, pool, sp = cx.pe, cx.act, cx.dve, cx.pool, cx.sp
        sb = lambda st, n, s, d: st.enter_context(nc.sbuf_tensor("t_" + n, s, d))
        ps = lambda st, n, s, d: st.enter_context(nc.psum_tensor("p_" + n, s, d))

        par = sb(es, "par", [128, P_N], F32)
        ident = sb(es, "ident", [128, 128], BF16)
        ones128 = sb(es, "ones128", [128, 128], BF16)
        onesblk = sb(es, "onesblk", [128, 128], BF16)
        pw3 = ExitStack()
        wbr = sb(pw3, "wbr", [128, 12, D], BF16)
        wout = sb(pw3, "wout", [128, 8, D], BF16)
        b_wbr, b_wout = Buf("wbr"), Buf("wout")
        pm = ExitStack()
        MKT = sb(pm, "MKT", [128, 4, NMEM], BF16)
        MV = sb(pm, "MV", [128, 4, 512], BF16)
        b_par, b_ident, b_ones, b_mkt, b_mv = Buf("par"), Buf("ident"), Buf("ones"), Buf("mkt"), Buf("mv")
        sl_c = cx.slot()
        cx.dma(sp, sl_c, par[:, 0:P_NEGB], par_d[:, 0:P_NEGB], writes=[b_par])
        cx.dma(sp, sl_c, ident[:], ident_d[:, :], writes=[b_ident])
        cx.op(dve, lambda h: h.memset(ones128[:], 1.0), writes=[b_ones])
        cx.op(dve, lambda h: h.memset(onesblk[:], 0.0), writes=[b_ones])
        cx.op(dve, lambda h: h.memset(onesblk[0:64, 0:64], 1.0), writes=[b_ones])
        cx.op(dve, lambda h: h.memset(onesblk[64:128, 64:128], 1.0), writes=[b_ones])
        cx.op(dve, lambda h: h.tensor_scalar(out=par[:, P_NEGB:P_NEGB + 24], in0=par[:, P_BG:P_BG + 24],
                                             scalar1=-1.0, scalar2=None, op0=ALU.mult), reads=[b_par], writes=[b_par])
        cx.op(act, lambda h: h.activation(out=par[:, P_SINKE:P_SINKE + 4], in_=par[:, P_SINK:P_SINK + 4],
                                          func=AF.Exp, bias=-SHIFT_A), reads=[b_par], writes=[b_par])

        p01 = ExitStack()
        hT = sb(p01, "hT", [128, 8, NT], BF16)
        mhT = sb(p01, "mhT", [128, 8, NMEM], BF16)
        b_hT = [Buf("hT%d" % t) for t in range(NTT)]
        b_mhT = Buf("mhT")

        p0 = ExitStack()
        NXT = 3
        xt = [sb(p0, "xt%d" % i, [128, 4, D], F32) for i in range(NXT)]
        b_xt = [Buf("xt%d" % i) for i in range(NXT)]
        sl_xt = [cx.slot(), cx.slot()]
        xn = [sb(p0, "xn%d" % i, [128, D], BF16) for i in range(2)]
        b_xn = [Buf("xn0"), Buf("xn1")]
        sqj = sb(p0, "sqj", [128, D], BF16)
        b_sqj = Buf("sqj")
        stt = sb(p0, "stt", [128, 3, 36], F32)
        b_st = [Buf("st%d" % j) for j in range(36)]
        pT = [ps(p0, "pT%d" % i, [128, 8, 128], BF16) for i in range(2)]
        b_pT = [Buf("pT0"), Buf("pT1")]

        def x_src(t):
            if t < NTT:
                return x_d[t * TT:(t + 1) * TT, :].rearrange("(s p) d -> p s d", p=128)
            return mem_d[:, :].rearrange("(s p) d -> p s d", p=128)

        cx.dma(sp, None, xt[0][:], x_src(0), writes=[b_xt[0]])
        cx.dma(sp, None, xt[1][:], x_src(1), writes=[b_xt[1]])
        for t in range(NTT + 1):
            if t + 2 <= NTT:
                cx.dma(sp, None, xt[(t + 2) % NXT][:], x_src(t + 2), writes=[b_xt[(t + 2) % NXT]])
            xb, bxb = xt[t % NXT], b_xt[t % NXT]
            bst = b_st[t]
            for s in range(4):
                j = t * 4 + s
                cx.op(act, lambda h: h.activation(out=sqj[:], in_=xb[:, s, :], func=AF.Square,
                                                  accum_out=stt[:, 0, j:j + 1]),
                      reads=[bxb], writes=[b_sqj, bst])
            cx.op(act, lambda h: h.activation(out=stt[:, 1, 4 * t:4 * t + 4], in_=stt[:, 0, 4 * t:4 * t + 4], func=AF.Ln,
                                              scale=1.0 / D, bias=EPS), reads=[bst], writes=[bst])
            cx.op(act, lambda h: h.activation(out=stt[:, 2, 4 * t:4 * t + 4], in_=stt[:, 1, 4 * t:4 * t + 4], func=AF.Exp,
                                              scale=-0.5), reads=[bst], writes=[bst])
            for s in range(4):
                j = t * 4 + s
                xnb, bxn = xn[j % 2], b_xn[j % 2]
                ptb, bpt = pT[j % 2], b_pT[j % 2]
                if s % 2 == 0:
                    cx.op(act, lambda h: h.activation(out=xnb[:], in_=xb[:, s, :], func=AF.Copy,
                                                      scale=stt[:, 2, j:j + 1]), reads=[bxb, bst], writes=[bxn])
                else:
                    cx.op(dve, lambda h: h.tensor_scalar(out=xnb[:], in0=xb[:, s, :], scalar1=stt[:, 2, j:j + 1],
                                                         scalar2=None, op0=ALU.mult), reads=[bxb, bst], writes=[bxn])

                def tr(h):
                    for kc in range(8):
                        i = h.transpose(out=ptb[:, kc, :], in_=xnb[:, kc * 128:(kc + 1) * 128], identity=ident[:])
                    return i
                cx.op(pe, tr, reads=[bxn, b_ident], writes=[bpt])
                if t < NTT:
                    dst, bd, gcol = hT[:, :, j * 128:(j + 1) * 128], b_hT[t], P_GMIX
                else:
                    dst, bd, gcol = mhT[:, :, s * 128:(s + 1) * 128], b_mhT, P_GMEM
                cx.op(dve, lambda h: h.tensor_tensor(out=dst, in0=ptb[:],
                                                     in1=par[:, gcol:gcol + 8].unsqueeze(2).to_broadcast([128, 8, 128]),
                                                     op=ALU.mult), reads=[bpt, b_par], writes=[bd])
        cx.barrier()
        p0.close()

        p1 = ExitStack()
        NW = 3
        wt = [sb(p1, "wt%d" % i, [128, 8, 512], BF16) for i in range(NW)]
        b_wt = [Buf("wt%d" % i) for i in range(NW)]
        sl_wt = [cx.slot() for _ in range(NW)]
        wmk = sb(p1, "wmk", [128, 8, 1024], BF16)
        b_wmk = Buf("wmk")
        sl_wmk = cx.slot()
        NSTG = 3
        stg = [sb(p1, "stg%d" % i, [128, NT], BF16) for i in range(NSTG)]
        b_stg = [Buf("stg%d" % i) for i in range(NSTG)]
        sl_stg = [cx.slot() for _ in range(NSTG)]
        vst = [sb(p1, "vst%d" % i, [128, 4, 512], BF16) for i in range(2)]
        b_vst = [Buf("vst0"), Buf("vst1")]
        sl_vst = [cx.slot(), cx.slot()]
        sqb = [sb(p1, "sqb%d" % i, [128, TT], BF16) for i in range(2)]
        b_sqb = [Buf("sqb0"), Buf("sqb1")]
        cpb = [sb(p1, "cpb%d" % i, [128, TT], BF16) for i in range(2)]
        b_cpb = [Buf("cpb0"), Buf("cpb1")]
        lnb = [sb(p1, "lnb%d" % i, [128, TT], F32) for i in range(2)]
        b_lnb = [Buf("lnb0"), Buf("lnb1")]
        NA = 5
        acc = [ps(p1, "acc%d" % i, [128, 512], F32) for i in range(NA)]
        b_acc = [Buf("acc%d" % i) for i in range(NA)]
        ssp = [ps(p1, "ssp%d" % i, [128, 512], F32) for i in range(2)]
        b_ssp = [Buf("ssp0"), Buf("ssp1")]

        def wsrc(c0, n):
            return w_in_d[:, c0:c0 + n].rearrange("(kc p) c -> p kc c", p=128)

        wtiles = []
        wtiles.append([(0, C_KA, 128),
                       (128, C_VA, 64), (192, C_VA, 64), (256, C_VA + 64, 64), (320, C_VA + 64, 64)])
        wtiles.append([(0, C_QA, 512)])
        for g in range(3):
            wtiles.append([(0, C_KB + 512 * g, 512)])
        for g in range(3):
            wtiles.append([(0, C_VB + 512 * g, 512)])
        for g in range(3):
            wtiles.append([(0, C_QB + 512 * g, 512)])
        wtiles.append([(0, C_QM, 512)])
        for i in range(6):
            wtiles.append([(0, C_GT + 512 * i, 512)])

        def load_w(k):
            i = k % NW
            for (dc, c0, n) in wtiles[k]:
                cx.dma(pool, sl_wt[i], wt[i][:, :, dc:dc + n], wsrc(c0, n), writes=[b_wt[i]])

        units = []
        stg_ctr = [0]
        vst_ctr = [0]

        def add_feat(k, wcol, dest, dil, gcol, hd, rhs_t, rhs_b, ntok_tiles, direct=None):
            si = stg_ctr[0] % NSTG
            if direct is None:
                stg_ctr[0] += 1
            for t in range(ntok_tiles):
                units.append(dict(kind="f", k=k, wcol=wcol, dest=dest, dil=dil, gcol=gcol, hd=hd, t=t,
                                  rhs=rhs_t, rhs_b=rhs_b[t] if isinstance(rhs_b, list) else rhs_b,
                                  si=si, last=(t == ntok_tiles - 1), direct=direct))

        def add_gate(k, wcol, j):
            si = stg_ctr[0] % NSTG
            stg_ctr[0] += 1
            for t in range(NTT):
                units.append(dict(kind="g", k=k, wcol=wcol, dest=GT[j], j=j, t=t, rhs=hT, rhs_b=b_hT[t],
                                  si=si, last=(t == NTT - 1)))

        def add_v(k, wcol, ncols, dest_fn):
            for t in range(NTT):
                vi = vst_ctr[0] % 2
                vst_ctr[0] += 1
                for s in range(4):
                    units.append(dict(kind="v", k=k, wcol=wcol, ncols=ncols, t=t, s=s, vi=vi, dest=dest_fn(t),
                                      last=(s == 3)))

        for hh in range(4):
            units.append(dict(kind="f", k=-1, wcol=128 * hh, dest=None, dil=1, gcol=P_GKM, hd=128, t=0,
                              rhs=mhT, rhs_b=b_mhT, si=0, last=False, direct=MKT[:, hh, :]))
        for s in range(4):
            units.append(dict(kind="v", k=-1, wcol=512, ncols=512, t=0, s=s, vi=0, dest=None, last=False,
                              direct=MV[:, s, :]))

        add_feat(0, 0, KA, 1, P_GKA, 64, hT, b_hT, NTT)
        add_v(0, 128, 256, lambda t: VA[t * TT:(t + 1) * TT, :].rearrange("(s p) c -> p s c", p=128))
        k = 1
        for c in range(4):
            add_feat(k, 128 * c, QA[c], 1, P_GQA, 64, hT, b_hT, NTT)
        k += 1
        for g in range(3):
            for hh in range(4):
                add_feat(k, 128 * hh, KB[g * 4 + hh], B_DIL[g], P_GKB, 128, hT, b_hT, NTT)
            k += 1
        for g in range(3):
            add_v(k, 0, 512, (lambda g: lambda t: VB[g, t * TT:(t + 1) * TT, :].rearrange("(s p) c -> p s c", p=128))(g))
            k += 1
        for g in range(3):
            for hh in range(4):
                add_feat(k, 128 * hh, QB[g * 4 + hh], B_DIL[g], P_GQB, 128, hT, b_hT, NTT)
            k += 1
        for hh in range(4):
            add_feat(k, 128 * hh, QM[hh], 1, P_GQM, 128, hT, b_hT, NTT)
        k += 1
        for i in range(6):
            for c in range(4):
                add_gate(k, 128 * c, i * 4 + c)
            k += 1
        NWT = k
        assert NWT == len(wtiles)
        cx.dma(pool, sl_wmk, wmk[:], w_mkv_d[:, :].rearrange("(kc p) c -> p kc c", p=128), writes=[b_wmk])
        load_w(0)
        load_w(1)
        loaded = 2

        def wbuf(u):
            if u["k"] < 0:
                return wmk, b_wmk
            return wt[u["k"] % NW], b_wt[u["k"] % NW]

        def emit_proj(ui, u):
            nonlocal loaded
            kk = u["k"]
            while kk >= 0 and loaded < NWT and loaded <= kk + 2:
                load_w(loaded)
                loaded += 1
            w, bw = wbuf(u)
            a, ba = acc[ui % NA], b_acc[ui % NA]
            t = u["t"]
            if u["kind"] in ("f", "g"):
                rhs = u["rhs"]

                def mm(h):
                    for kc in range(8):
                        i = h.matmul(a[:, :], lhsT=w[:, kc, u["wcol"]:u["wcol"] + 128],
                                     rhs=rhs[:, kc, t * TT:(t + 1) * TT], start=(kc == 0), stop=(kc == 7))
                    return i
                cx.op(pe, mm, reads=[bw, u["rhs_b"]], writes=[ba])
            else:
                src, bsrc = (hT, b_hT[t]) if kk >= 0 else (mhT, b_mhT)
                tok0 = t * TT + u["s"] * 128
                nco = u["ncols"]

                def mm(h):
                    for kc in range(8):
                        i = h.matmul(a[:, 0:nco], lhsT=src[:, kc, tok0:tok0 + 128],
                                     rhs=w[:, kc, u["wcol"]:u["wcol"] + nco], start=(kc == 0), stop=(kc == 7))
                    return i
                cx.op(pe, mm, reads=[bw, bsrc], writes=[ba])

        def emit_first(ui, u):
            a, ba = acc[ui % NA], b_acc[ui % NA]
            if u["kind"] == "f":
                q, bq = sqb[ui % 2], b_sqb[ui % 2]
                c_, bc = cpb[ui % 2], b_cpb[ui % 2]
                cx.op(dve, lambda h: h.tensor_copy(out=c_[:], in_=a[:, :]), reads=[ba], writes=[bc])
                cx.op(dve, lambda h: h.tensor_tensor(out=q[:], in0=c_[:], in1=c_[:], op=ALU.mult),
                      reads=[bc], writes=[bq])
            elif u["kind"] == "g":
                j = u["j"]
                s_, bs = stg[u["si"]], b_stg[u["si"]]
                t = u["t"]
                cx.op(act, lambda h: h.activation(out=s_[:, t * TT:(t + 1) * TT], in_=a[:, :], func=AF.Sigmoid,
                                                  bias=par[:, P_BG + j:P_BG + j + 1]),
                      reads=[ba, b_par], writes=[bs])
            else:
                nco = u["ncols"]
                if u.get("direct") is not None:
                    cx.op(dve, lambda h: h.tensor_copy(out=u["direct"], in_=a[:, 0:nco]), reads=[ba], writes=[b_mv])
                else:
                    v, bv = vst[u["vi"]], b_vst[u["vi"]]
                    cx.op(dve, lambda h: h.tensor_copy(out=v[:, u["s"], 0:nco], in_=a[:, 0:nco]),
                          reads=[ba], writes=[bv])
                    if u["last"]:
                        cx.dma(sp, sl_vst[u["vi"]], u["dest"], v[:, :, 0:nco], reads=[bv])

        def emit_rest(ui, u):
            a, ba = acc[ui % NA], b_acc[ui % NA]
            t = u["t"]
            if u["kind"] == "f":
                q, bq = sqb[ui % 2], b_sqb[ui % 2]
                sp_, bsp = ssp[ui % 2], b_ssp[ui % 2]
                l, bl = lnb[ui % 2], b_lnb[ui % 2]
                om = ones128 if u["hd"] == 128 else onesblk
                cx.op(pe, lambda h: h.matmul(sp_[:, :], lhsT=om[:], rhs=q[:], start=True, stop=True),
                      reads=[bq, b_ones], writes=[bsp])
                cx.op(act, lambda h: h.activation(out=l[:], in_=sp_[:, :], func=AF.Ln, scale=1.0 / u["hd"], bias=EPS),
                      reads=[bsp], writes=[bl])
                cx.op(act, lambda h: h.activation(out=l[:], in_=l[:], func=AF.Exp, scale=-0.5),
                      reads=[bl], writes=[bl])
                d = u["dil"]
                gcol = u["gcol"]
                if u.get("direct") is not None:
                    cx.op(dve, lambda h: h.scalar_tensor_tensor(out=u["direct"], in0=a[:, :], scalar=par[:, gcol:gcol + 1],
                                                                in1=l[:], op0=ALU.mult, op1=ALU.mult),
                          reads=[ba, bl, b_par], writes=[b_mkt])
                    return
                s_, bs = stg[u["si"]], b_stg[u["si"]]
                if d == 1:
                    o_ap, a_ap, l_ap = s_[:, t * TT:(t + 1) * TT], a[:, :], l[:]
                else:
                    npl = TT // d
                    o_ap = s_[:, :].rearrange("p (r pos) -> p r pos", r=d)[:, :, t * npl:(t + 1) * npl]
                    a_ap = a[:, :].rearrange("p (pl r) -> p r pl", r=d)
                    l_ap = l[:, :].rearrange("p (pl r) -> p r pl", r=d)
                cx.op(dve, lambda h: h.scalar_tensor_tensor(out=o_ap, in0=a_ap, scalar=par[:, gcol:gcol + 1],
                                                            in1=l_ap, op0=ALU.mult, op1=ALU.mult),
                      reads=[ba, bl, b_par], writes=[bs])
                if u["last"]:
                    cx.dma(sp, sl_stg[u["si"]], u["dest"], s_[:], reads=[bs])
            elif u["kind"] == "g":
                s_, bs = stg[u["si"]], b_stg[u["si"]]
                if u["last"]:
                    cx.dma(sp, sl_stg[u["si"]], u["dest"], s_[:], reads=[bs])

        for ui in range(len(units) + 1):
            if ui < len(units):
                emit_proj(ui, units[ui])
                emit_first(ui, units[ui])
            if ui >= 1:
                emit_rest(ui - 1, units[ui - 1])
        cx.barrier()
        p1.close()
        p01.close()

        if debug == 1:
            pm.close()
            pw3.close()
            return nc

        for i in range(3):
            cx.dma(pool, None, wbr[:, 4 * i:4 * i + 4, :],
                   w_br_d[512 * i:512 * (i + 1), :].rearrange("(kc p) n -> p kc n", p=128), writes=[b_wbr])
        for i in range(2):
            cx.dma(pool, None, wout[:, 4 * i:4 * i + 4, :],
                   w_out_d[512 * i:512 * (i + 1), :].rearrange("(kc p) n -> p kc n", p=128), writes=[b_wout])
        p2 = ExitStack()
        NS = 4
        sbank = [ps(p2, "S%d" % i, [128, 512], F32) for i in range(NS)]
        b_sbank = [Buf("S%d" % i) for i in range(NS)]
        nbank = [ps(p2, "N%d" % i, [128, 512], F32) for i in range(2)]
        b_nbank = [Buf("N%d" % i) for i in range(2)]
        dbank = [ps(p2, "Dn%d" % i, [128, 512], F32) for i in range(2)]
        b_dbank = [Buf("Dn%d" % i) for i in range(2)]
        ebuf = [sb(p2, "E%d" % i, [128, 512], BF16) for i in range(NS)]
        b_ebuf = [Buf("E%d" % i) for i in range(NS)]
        ptbuf = [sb(p2, "PT%d" % i, [128, 512], BF16) for i in range(NS)]
        b_ptbuf = [Buf("PT%d" % i) for i in range(NS)]
        rD = [sb(p2, "rD%d" % i, [128, 512], F32) for i in range(2)]
        b_rD = [Buf("rD0"), Buf("rD1")]
        ost = [sb(p2, "ost%d" % i, [128, 4, 512], BF16) for i in range(2)]
        b_ost = [Buf("ost0"), Buf("ost1")]
        sl_ost = [cx.slot(), cx.slot()]
        qt_ = [sb(p2, "qt%d" % i, [128, 4, 512], BF16) for i in range(2)]
        b_qt = [Buf("qt0"), Buf("qt1")]
        sl_qt = [cx.slot(), cx.slot()]

        def attn_step(items, i, n, LAG):
            if i < n:
                it = items[i]
                r3 = i % NS
                nq = it["nq"]
                S, bS = sbank[r3], b_sbank[r3]
                if "s_list" in it:
                    def smm(h):
                        for (k_ap, q_ap, off, n1) in it["s_list"]:
                            ins = h.matmul(S[:, off:off + n1], lhsT=k_ap, rhs=q_ap, start=True, stop=True,
                                           skip_group_check=True)
                        return ins
                    cx.op(pe, smm, reads=it["kq_bufs"], writes=[bS])
                else:
                    cx.op(pe, lambda h: h.matmul(it["s_view"](S), lhsT=it["k"], rhs=it["q"], start=True, stop=True),
                          reads=it["kq_bufs"], writes=[bS])
                P_, bP = ptbuf[r3], b_ptbuf[r3]
                if it["mask"] is None:
                    cx.op(act, lambda h: h.activation(out=P_[:, 0:nq], in_=S[:, 0:nq], func=AF.Exp,
                                                      scale=it["scale"], bias=it["shift"]),
                          reads=[bS], writes=[bP])
                else:
                    E_, bE = ebuf[r3], b_ebuf[r3]
                    cx.op(act, lambda h: h.activation(out=E_[:, 0:nq], in_=S[:, 0:nq], func=AF.Exp,
                                                      scale=it["scale"], bias=it["shift"]),
                          reads=[bS], writes=[bE])
                    me = pool if (it.get("pool_share") and i % 4 == 3) else dve
                    cx.op(me, lambda h: h.tensor_tensor(out=it["s_view"](P_), in0=it["s_view"](E_), in1=it["mask"],
                                                        op=ALU.mult),
                          reads=[bE] + it["mask_bufs"], writes=[bP])
            if i >= LAG:
                it = items[i - LAG]
                r3 = (i - LAG) % NS
                P_, bP = ptbuf[r3], b_ptbuf[r3]

                def pv(h):
                    for ent in it["pv"]:
                        if len(ent) == 2:
                            o_ap, v_ap = ent
                            r_ap = it["s_view"](P_)
                        else:
                            o_ap, v_ap, off, n1 = ent
                            r_ap = P_[:, off:off + n1]
                        st_flag = it["start"] and (len(ent) == 2 or off == 0)
                        ins = h.matmul(o_ap, lhsT=v_ap, rhs=r_ap, start=st_flag, stop=it["stop"],
                                       skip_group_check=True)
                    return ins
                for pp in list(pending):
                    if pp[2] is not it.get("grp_id") and pp[3] == it.get("bank"):
                        pending.remove(pp)
                        pp[1]()
                cx.op(pe, pv, reads=[bP] + it["v_bufs"], writes=it["out_bufs"])
                if it.get("post") is not None:
                    pending.append((i + POST_DELAY, it["post"], it.get("grp_id"), it.get("bank")))

        pending = []
        POST_DELAY = 0

        def attn_pipeline_hook(items, hook):
            n = len(items)
            LAG = NS - 1
            for i in range(n + LAG):
                if i < n and i in hook:
                    hook[i]()
                for pp in list(pending):
                    if pp[0] <= i:
                        pending.remove(pp)
                        pp[1]()
                attn_step(items, i, n, LAG)
            for pp in list(pending):
                pending.remove(pp)
                pp[1]()

        def flat_view(nq):
            return lambda T: T[:, 0:nq]

        def pair_view(nq):
            return lambda T: T[:, 0:2 * nq].rearrange("p (b q) -> p b q", b=2)

        p2a = ExitStack()
        KAr = sb(p2a, "KAr", [128, 2, NT], BF16)
        VAd = sb(p2a, "VAd", [128, 32, 256], BF16)
        WAt = sb(p2a, "WAt", [128, 3, 8, 384], BF16)
        qtz = [sb(p2a, "qtz%d" % i, [128, 4, 2, 256], BF16) for i in range(2)]
        b_qtz = [Buf("qtz0"), Buf("qtz1")]
        b_kar, b_var, b_wat = Buf("KAr"), Buf("VAr"), Buf("WAt")
        for kvh in range(2):
            for hf in range(2):
                cx.dma(sp, None, KAr[64 * hf:64 * hf + 64, kvh, :], KA[64 * kvh:64 * kvh + 64, :], writes=[b_kar])
        cx.dma(sp, None, VAd[:], VA[:, :].rearrange("(c p) n -> p c n", p=128), writes=[b_var])
        cx.dma(sp, None, WAt[:], wa_d[:, :, :, :], writes=[b_wat])
        for i in range(2):
            cx.op(dve, lambda h: h.memset(qtz[i][0:64, :, 1, :], 0.0), writes=[b_qtz[i]])
            cx.op(dve, lambda h: h.memset(qtz[i][64:128, :, 0, :], 0.0), writes=[b_qtz[i]])
        QTA = 256

        def load_qz(qt):
            i = qt % 2
            q0 = qt * QTA
            cx.dma(sp, None, qtz[i][0:64, :, 0, :], QA[0:4, 0:64, q0:q0 + QTA].rearrange("c p t -> p c t"),
                   writes=[b_qtz[i]])
            cx.dma(sp, None, qtz[i][64:128, :, 1, :], QA[0:4, 64:128, q0:q0 + QTA].rearrange("c p t -> p c t"),
                   writes=[b_qtz[i]])

        items = []
        hook = {}
        grp = [0]

        def a_post(qt, c, gi):
            def f():
                nb, dbk = nbank[gi % 2], dbank[gi % 2]
                r_, br = rD[gi % 2], b_rD[gi % 2]
                o_, bo = ost[qt % 2], b_ost[qt % 2]
                cx.op(act, lambda h: h.activation(out=r_[:], in_=dbk[:, :], func=AF.Ln,
                                                  bias=par[:, P_SINKE + c:P_SINKE + c + 1]),
                      reads=[b_dbank[gi % 2], b_par], writes=[br])
                cx.op(act, lambda h: h.activation(out=r_[:], in_=r_[:], func=AF.Exp, scale=-1.0), reads=[br], writes=[br])
                cx.op(dve, lambda h: h.tensor_tensor(out=o_[0:64, c, 0:QTA], in0=nb[0:64, 0:QTA], in1=r_[0:64, 0:QTA],
                                                     op=ALU.mult),
                      reads=[b_nbank[gi % 2], br], writes=[bo])
                cx.op(dve, lambda h: h.tensor_tensor(out=o_[64:128, c, 0:QTA], in0=nb[64:128, QTA:2 * QTA],
                                                     in1=r_[64:128, QTA:2 * QTA], op=ALU.mult),
                      reads=[b_nbank[gi % 2], br], writes=[bo])
                if c == 3:
                    cx.dma(sp, None, OT[0:4, :, qt * QTA:(qt + 1) * QTA].rearrange("c p t -> p c t"), o_[:, :, 0:QTA],
                           reads=[bo])
            return f

        load_qz(0)
        for qt in range(NT // QTA):
            q0 = qt * QTA
            if qt + 1 < NT // QTA:
                hook[len(items)] = (lambda q: (lambda: load_qz(q)))(qt + 1)
            for c in range(4):
                kvh = c // 2
                gi = grp[0]
                grp[0] += 1
                kcs = list(range(max(0, q0 // 128 - 1), min(32, q0 // 128 + 3)))
                nv = nbank[gi % 2][:, :].rearrange("p (b q) -> p b q", b=2)
                dv = dbank[gi % 2][:, :].rearrange("p (b q) -> p b q", b=2)
                for kc in kcs:
                    qs, qe = max(q0, 128 * kc - 128), min(q0 + QTA, 128 * kc + 256)
                    nq = qe - qs
                    var = 1 if kc == 15 else (2 if kc == 16 else 0)
                    b0 = qs - (128 * kc - 128)
                    items.append(dict(
                        nq=2 * nq, s_view=pair_view(nq), k=KAr[:, kvh, 128 * kc:128 * kc + 128],
                        q=qtz[qt % 2][:, c, :, qs - q0:qe - q0], kq_bufs=[b_kar, b_qtz[qt % 2]],
                        scale=0.125, shift=-SHIFT_A, mask=WAt[:, var, 2 * c:2 * c + 2, b0:b0 + nq], mask_bufs=[b_wat],
                        pv=([(nv[:, :, qs - q0:qe - q0], VAd[:, kc, kvh * 128:kvh * 128 + 128]),
                             (dv[:, :, qs - q0:qe - q0], ones128[:])] if nq == QTA else
                            [(nv[:, 0, qs - q0:qe - q0], VAd[:, kc, kvh * 128:kvh * 128 + 128], 0, nq),
                             (dv[:, 0, qs - q0:qe - q0], ones128[:], 0, nq),
                             (nv[:, 1, qs - q0:qe - q0], VAd[:, kc, kvh * 128:kvh * 128 + 128], nq, nq),
                             (dv[:, 1, qs - q0:qe - q0], ones128[:], nq, nq)]),
                        start=(kc == kcs[0]), stop=(kc == kcs[-1]), v_bufs=[b_var, b_ones],
                        out_bufs=[b_nbank[gi % 2], b_dbank[gi % 2]], grp_id=gi, bank=gi % 2))
                items[-1]["post"] = a_post(qt, c, gi)
        attn_pipeline_hook(items, hook)
        cx.barrier()
        p2a.close()

        p2b = ExitStack()
        WBt = sb(p2b, "WBt", [128, 3, 12, 256], BF16)
        b_wbt = Buf("WBt")
        cx.dma(sp, None, WBt[:], wb_d[:, :, :, :], writes=[b_wbt])
        Nacc = sb(p2b, "Nacc", [128, 4, 2048], F32)
        Dacc = sb(p2b, "Dacc", [128, 4, 2048], F32)
        b_nacc = [Buf("Nacc%d" % i) for i in range(4)]
        b_dacc = [Buf("Dacc%d" % i) for i in range(4)]
        RB = 4
        ksl = [sb(p2b, "ksl%d" % i, [128, 4, 512], BF16) for i in range(RB)]
        vsl = [sb(p2b, "vsl%d" % i, [128, 4, 512], BF16) for i in range(RB)]
        qtb = [sb(p2b, "qtb%d" % i, [128, 4, 256], BF16) for i in range(RB)]
        b_ksl = [Buf("ksl%d" % i) for i in range(RB)]
        b_vsl = [Buf("vsl%d" % i) for i in range(RB)]
        b_qtb = [Buf("qtb%d" % i) for i in range(RB)]
        def load_q(src, qt, nq=4):
            i = qt % 2
            cx.dma(sp, None, qt_[i][:], src[0:nq, :, qt * TT:(qt + 1) * TT].rearrange("c p t -> p c t"),
                   writes=[b_qt[i]])

        items = []
        hook = {}
        load_q(QM, 0)
        sc_m = float(128 ** -0.5)

        def m_post(qt, hh, gi):
            def f():
                nb, dbk = nbank[gi % 2], dbank[gi % 2]
                r_, br = rD[gi % 2], b_rD[gi % 2]
                o_, bo = ost[qt % 2], b_ost[qt % 2]
                cx.op(act, lambda h: h.activation(out=r_[:], in_=dbk[:, :], func=AF.Ln),
                      reads=[b_dbank[gi % 2]], writes=[br])
                cx.op(act, lambda h: h.activation(out=r_[:], in_=r_[:], func=AF.Exp, scale=-1.0), reads=[br], writes=[br])
                cx.op(dve, lambda h: h.tensor_tensor(out=o_[:, hh, :], in0=nb[:, :], in1=r_[:], op=ALU.mult),
                      reads=[b_nbank[gi % 2], br], writes=[bo])
                if hh == 3:
                    cx.dma(sp, sl_ost[qt % 2], OT[8:12, :, qt * TT:(qt + 1) * TT].rearrange("c p t -> p c t"), o_[:],
                           reads=[bo])
            return f

        for qt in range(NTT):
            st_ = qt // 4
            if qt + 1 < NTT:
                hook[len(items)] = (lambda q: (lambda: load_q(QM, q)))(qt + 1)
            for hh in range(4):
                gi = grp[0]
                grp[0] += 1
                for j in range(2):
                    items.append(dict(
                        nq=TT, s_view=flat_view(TT), k=MKT[:, hh, 256 * st_ + 128 * j:256 * st_ + 128 * j + 128], q=qt_[qt % 2][:, hh, :],
                        kq_bufs=[b_mkt, b_qt[qt % 2]], scale=sc_m, shift=-SHIFT_B, mask=None, mask_bufs=[],
                        pv=[(nbank[gi % 2][:, :], MV[:, 2 * st_ + j, hh * 128:hh * 128 + 128]),
                            (dbank[gi % 2][:, :], ones128[:])],
                        start=(j == 0), stop=(j == 1), v_bufs=[b_mv, b_ones],
                        out_bufs=[b_nbank[gi % 2], b_dbank[gi % 2]], qt=qt, grp_id=gi, bank=gi % 2))
                items[-1]["post"] = m_post(qt, hh, gi)
        attn_pipeline_hook(items, hook)

        sc_b = float(128 ** -0.5)
        tile_ctr = [0]

        def hb_view(HB, nq):
            return lambda T: T[:, 0:HB * nq].rearrange("p (b q) -> p b q", b=HB)

        for S2 in range(2):
            items = []
            hook = {}
            for g in range(3):
                d = B_DIL[g]
                Ls = NT // d
                half = Ls // 2
                nqt = min(256, half)
                HB = 512 // nqt if nqt < 256 else 2
                HB = min(HB, 4)
                for r in range(d):
                    for p0 in range(S2 * half, (S2 + 1) * half, nqt):
                        ti = tile_ctr[0]
                        tile_ctr[0] += 1
                        kc_lo = max(0, (p0 - 64) // 128)
                        kc_hi = min(Ls // 128 - 1, (p0 + nqt - 1 + 64) // 128)
                        nk = kc_hi - kc_lo + 1
                        base = r * Ls

                        def loader(ti=ti, g=g, d=d, r=r, p0=p0, nqt=nqt, kc_lo=kc_lo, nk=nk, base=base):
                            i2 = ti % RB
                            cx.dma(sp, None, qtb[i2][:, :, 0:nqt],
                                   QB[4 * g:4 * g + 4, :, base + p0:base + p0 + nqt].rearrange("c p t -> p c t"),
                                   writes=[b_qtb[i2]])
                            cx.dma(sp, None, ksl[i2][:, :, 0:nk * 128],
                                   KB[4 * g:4 * g + 4, :, base + 128 * kc_lo:base + 128 * (kc_lo + nk)].rearrange(
                                       "c p t -> p c t"), writes=[b_ksl[i2]])
                            row0 = 128 * kc_lo * d + r
                            nrow = nk * 128
                            if d == 1:
                                src = VB[g, row0:row0 + nrow, :]
                            else:
                                src = VB[g, row0:row0 + (nrow - 1) * d + 1:d, :]
                            cx.dma(sp, None, vsl[i2][:, 0:nk, :], src.rearrange("(kc a) c -> a kc c", a=128),
                                   writes=[b_vsl[i2]])
                        hook.setdefault(len(items), []).append(loader)
                        i2 = ti % RB
                        nat0 = (p0 - S2 * half) * d + r
                        for h0 in range(0, 4, HB):
                            gi = grp[0]
                            grp[0] += 1
                            nv = nbank[gi % 2][:, 0:HB * nqt].rearrange("p (b q) -> p b q", b=HB)
                            dv = dbank[gi % 2][:, 0:HB * nqt].rearrange("p (b q) -> p b q", b=HB)
                            its = []
                            for kc in range(kc_lo, kc_hi + 1):
                                qs, qe = max(p0, 128 * kc - 64), min(p0 + nqt, 128 * kc + 192)
                                nq = qe - qs
                                if nq <= 0:
                                    continue
                                var = 1 if kc == Ls // 256 - 1 else (2 if kc == Ls // 256 else 0)
                                b0_ = qs - (128 * kc - 64)
                                s_list = []
                                pvl = []
                                for hb in range(HB):
                                    hh = h0 + hb
                                    s_list.append((ksl[i2][:, hh, (kc - kc_lo) * 128:(kc - kc_lo) * 128 + 128],
                                                   qtb[i2][:, hh, qs - p0:qe - p0], hb * nq, nq))
                                    pvl.append((nv[:, hb, qs - p0:qe - p0], vsl[i2][:, kc - kc_lo, hh * 128:hh * 128 + 128],
                                                hb * nq, nq))
                                    pvl.append((dv[:, hb, qs - p0:qe - p0], ones128[:], hb * nq, nq))
                                its.append(dict(
                                    nq=HB * nq, s_view=hb_view(HB, nq), s_list=s_list,
                                    kq_bufs=[b_ksl[i2], b_qtb[i2]],
                                    scale=sc_b, shift=-SHIFT_B,
                                    mask=WBt[:, var, 4 * g + h0:4 * g + h0 + HB, b0_:b0_ + nq], mask_bufs=[b_wbt],
                                    pv=pvl, start=False, stop=False, v_bufs=[b_vsl[i2], b_ones], pool_share=True,
                                    out_bufs=[b_nbank[gi % 2], b_dbank[gi % 2]], grp_id=gi, bank=gi % 2))
                            its[0]["start"] = True
                            its[-1]["stop"] = True

                            def b_post(gi=gi, h0=h0, HB=HB, g=g, d=d, nat0=nat0, nqt=nqt):
                                nv_ = nbank[gi % 2][:, 0:HB * nqt].rearrange("p (b q) -> p b q", b=HB)
                                dv_ = dbank[gi % 2][:, 0:HB * nqt].rearrange("p (b q) -> p b q", b=HB)
                                if d == 1:
                                    na = Nacc[:, h0:h0 + HB, nat0:nat0 + nqt]
                                    da = Dacc[:, h0:h0 + HB, nat0:nat0 + nqt]
                                else:
                                    na = Nacc[:, h0:h0 + HB, nat0:nat0 + (nqt - 1) * d + 1:d]
                                    da = Dacc[:, h0:h0 + HB, nat0:nat0 + (nqt - 1) * d + 1:d]
                                bn_ = [b_nacc[h0 + k_] for k_ in range(HB)]
                                bd_ = [b_dacc[h0 + k_] for k_ in range(HB)]
                                if g == 0:
                                    cx.op(act, lambda h: h.activation(out=na, in_=nv_, func=AF.Copy),
                                          reads=[b_nbank[gi % 2]], writes=bn_)
                                    cx.op(act, lambda h: h.activation(out=da, in_=dv_, func=AF.Copy),
                                          reads=[b_dbank[gi % 2]], writes=bd_)
                                else:
                                    cx.op(dve, lambda h: h.tensor_tensor(out=na, in0=nv_, in1=na, op=ALU.add),
                                          reads=[b_nbank[gi % 2]] + bn_, writes=bn_)
                                    r_, br = rD[gi % 2], b_rD[gi % 2]
                                    rv = r_[:, 0:HB * nqt].rearrange("p (b q) -> p b q", b=HB)
                                    cx.op(act, lambda h: h.activation(out=rv, in_=dv_, func=AF.Copy),
                                          reads=[b_dbank[gi % 2]], writes=[br])
                                    cx.op(pool, lambda h: h.tensor_tensor(out=da, in0=rv, in1=da, op=ALU.add),
                                          reads=[br] + bd_, writes=bd_)
                            its[-1]["post"] = b_post
                            items.extend(its)
            starts = sorted(hook)
            hook2 = {}
            for si, st in enumerate(starts):
                at = (starts[si - RB + 1] + NS - 1) if si >= RB else -1
                assert at < st
                hook2.setdefault(at, []).extend(hook[st])
            for f in hook2.get(-1, []):
                f()
            hk = {k_: (lambda fs: (lambda: [f() for f in fs]))(v) for k_, v in hook2.items() if k_ >= 0}
            attn_pipeline_hook(items, hk)
            for hh in range(4):
                cx.op(act, lambda h: h.activation(out=Dacc[:, hh, :], in_=Dacc[:, hh, :], func=AF.Ln),
                      reads=[b_dacc[hh]], writes=[b_dacc[hh]])
                cx.op(act, lambda h: h.activation(out=Dacc[:, hh, :], in_=Dacc[:, hh, :], func=AF.Exp, scale=-1.0),
                      reads=[b_dacc[hh]], writes=[b_dacc[hh]])
            for q4 in range(4):
                o_, bo = ost[q4 % 2], b_ost[q4 % 2]
                for hh in range(4):
                    cx.op(pool, lambda h: h.tensor_tensor(out=o_[:, hh, :], in0=Nacc[:, hh, q4 * 512:(q4 + 1) * 512],
                                                          in1=Dacc[:, hh, q4 * 512:(q4 + 1) * 512], op=ALU.mult),
                          reads=[b_nacc[hh], b_dacc[hh]], writes=[bo])
                t0 = S2 * 2048 + q4 * 512
                cx.dma(sp, sl_ost[q4 % 2], OT[4:8, :, t0:t0 + 512].rearrange("c p t -> p c t"), o_[:], reads=[bo])
        cx.barrier()
        p2b.close()
        p2.close()
        if debug == 2:
            pm.close()
            pw3.close()
            return nc
        pm.close()

        p3 = ExitStack()
        oTt = [sb(p3, "oTt%d" % i, [128, 12, TT], BF16) for i in range(2)]
        b_oTt = [Buf("oTt0"), Buf("oTt1")]
        sl_oTt = [cx.slot(), cx.slot()]
        gTt = [sb(p3, "gTt%d" % i, [128, 24, TT], BF16) for i in range(2)]
        b_gTt = [Buf("gTt0"), Buf("gTt1")]
        xt3 = [sb(p3, "xt3_%d" % i, [128, 4, D], F32) for i in range(2)]
        b_xt3 = [Buf("xt3_0"), Buf("xt3_1")]
        sl_xt3 = [cx.slot(), cx.slot()]
        sl_y3 = [cx.slot(), cx.slot()]
        zT = [sb(p3, "zT%d" % i, [128, 8, TT], BF16) for i in range(2)]
        b_zT = [Buf("zT0"), Buf("zT1")]
        tb = [sb(p3, "tb%d" % i, [128, TT], F32) for i in range(6)]
        b_tb = [Buf("tb%d" % i) for i in range(6)]
        h2 = [sb(p3, "h2_%d" % i, [128, D], BF16) for i in range(2)]
        b_h2 = [Buf("h2_0"), Buf("h2_1")]
        h2Ts = [sb(p3, "h2Ts%d" % i, [128, 8, TT], BF16) for i in range(2)]
        b_h2Ts = [Buf("h2Ts0"), Buf("h2Ts1")]
        sl_h2Ts = [cx.slot(), cx.slot()]
        st3 = sb(p3, "st3", [128, 3, 32], F32)
        b_st3 = [Buf("st3_%d" % j) for j in range(32)]
        sqj3 = sb(p3, "sqj3", [128, D], BF16)
        b_sqj3 = Buf("sqj3")
        ybr = [ps(p3, "ybr%d" % i, [128, 512], F32) for i in range(4)]
        b_ybr = [Buf("ybr%d" % i) for i in range(4)]
        acc3 = [ps(p3, "acc3_%d" % i, [128, 512], F32) for i in range(2)]
        b_acc3 = [Buf("acc3_0"), Buf("acc3_1")]
        pT3 = [ps(p3, "pT3_%d" % i, [128, 8, 128], BF16) for i in range(2)]
        b_pT3 = [Buf("pT3_0"), Buf("pT3_1")]

        def p3_og_loads(t):
            i = t % 2
            cx.dma(sp, None, oTt[i][:], OT[:, :, t * TT:(t + 1) * TT].rearrange("c p t -> p c t"),
                   writes=[b_oTt[i]])
            cx.dma(sp, None, gTt[i][:], GT[:, :, t * TT:(t + 1) * TT].rearrange("c p t -> p c t"), writes=[b_gTt[i]])

        def p3_x_loads(t):
            i = t % 2
            cx.dma(sp, None, xt3[i][:], x_d[t * TT:(t + 1) * TT, :].rearrange("(s p) d -> p s d", p=128),
                   writes=[b_xt3[i]])

        yc = [0]

        def merge_step(t, dc):
            i = t % 2
            o_, bo = oTt[i], b_oTt[i]
            g_, bg = gTt[i], b_gTt[i]
            z_, bz = zT[i], b_zT[i]
            tset = (dc % 2) * 3
            for b in range(3):
                Y, bY = ybr[yc[0] % 4], b_ybr[yc[0] % 4]
                yc[0] += 1

                def mm(h):
                    for j in range(4):
                        ins = h.matmul(Y[:, :], lhsT=wbr[:, b * 4 + j, dc * 128:(dc + 1) * 128],
                                       rhs=o_[:, b * 4 + j, :], start=(j == 0), stop=(j == 3))
                    return ins
                cx.op(pe, mm, reads=[b_wbr, bo], writes=[bY])
                cx.op(dve, lambda h: h.tensor_tensor(out=tb[tset + b][:], in0=Y[:, :], in1=g_[:, b * 8 + dc, :],
                                                     op=ALU.mult),
                      reads=[bY, bg], writes=[b_tb[tset + b]])
            cx.op(pool, lambda h: h.tensor_tensor(out=tb[tset][:], in0=tb[tset][:], in1=tb[tset + 1][:], op=ALU.add),
                  reads=[b_tb[tset], b_tb[tset + 1]], writes=[b_tb[tset]])
            cx.op(pool, lambda h: h.tensor_tensor(out=z_[:, dc, :], in0=tb[tset][:], in1=tb[tset + 2][:], op=ALU.add),
                  reads=[b_tb[tset], b_tb[tset + 2]], writes=[bz])

        def wa_step(t, s_):
            i = t % 2
            x_, bx = xt3[i], b_xt3[i]
            z_, bz = zT[i], b_zT[i]
            j = t * 4 + s_
            for half in range(2):
                A, bA = acc3[half], b_acc3[half]

                def mm2(h):
                    for kc in range(8):
                        ins = h.matmul(A[:, :], lhsT=z_[:, kc, s_ * 128:(s_ + 1) * 128],
                                       rhs=wout[:, kc, half * 512:(half + 1) * 512], start=(kc == 0), stop=(kc == 7))
                    return ins
                cx.op(pe, mm2, reads=[bz, b_wout], writes=[bA])
                cx.op(dve, lambda h: h.tensor_tensor(out=x_[:, s_, half * 512:(half + 1) * 512], in0=A[:, :],
                                                     in1=x_[:, s_, half * 512:(half + 1) * 512], op=ALU.add),
                      reads=[bA, bx], writes=[bx])
            cx.op(act, lambda h: h.activation(out=sqj3[:], in_=x_[:, s_, :], func=AF.Square,
                                              accum_out=st3[:, 0, j:j + 1]), reads=[bx], writes=[b_sqj3, b_st3[j]])
            cx.op(act, lambda h: h.activation(out=st3[:, 1, j:j + 1], in_=st3[:, 0, j:j + 1], func=AF.Ln,
                                              scale=1.0 / D, bias=EPS), reads=[b_st3[j]], writes=[b_st3[j]])
            cx.op(act, lambda h: h.activation(out=st3[:, 2, j:j + 1], in_=st3[:, 1, j:j + 1], func=AF.Exp,
                                              scale=-0.5), reads=[b_st3[j]], writes=[b_st3[j]])
            hb_, bhb = h2[j % 2], b_h2[j % 2]
            cx.op(act, lambda h: h.activation(out=hb_[:], in_=x_[:, s_, :], func=AF.Copy,
                                              scale=st3[:, 2, j:j + 1]), reads=[bx, b_st3[j]], writes=[bhb])

        def wb_step(t, s_):
            i = t % 2
            j = t * 4 + s_
            hs, bhs = h2Ts[i], b_h2Ts[i]
            hb_, bhb = h2[j % 2], b_h2[j % 2]
            pt_, bpt = pT3[j % 2], b_pT3[j % 2]

            def tr3(h):
                for kc in range(8):
                    ins = h.transpose(out=pt_[:, kc, :], in_=hb_[:, kc * 128:(kc + 1) * 128], identity=ident[:])
                return ins
            cx.op(pe, tr3, reads=[bhb, b_ident], writes=[bpt])
            cx.op(dve, lambda h: h.tensor_tensor(out=hs[:, :, s_ * 128:(s_ + 1) * 128], in0=pt_[:],
                                                 in1=par[:, P_GMLP:P_GMLP + 8].unsqueeze(2).to_broadcast([128, 8, 128]),
                                                 op=ALU.mult), reads=[bpt, b_par], writes=[bhs])

        def p3_stores(t):
            i = t % 2
            cx.dma(sp, None, y_d[t * TT:(t + 1) * TT, :].rearrange("(s p) d -> p s d", p=128), xt3[i][:],
                   reads=[b_xt3[i]])
            cx.dma(sp, None, H2T[:, :, t * TT:(t + 1) * TT].rearrange("c p t -> p c t"), h2Ts[i][:],
                   reads=[b_h2Ts[i]])

        p3_og_loads(0)
        p3_x_loads(0)
        p3_og_loads(1)
        for dc in range(8):
            merge_step(0, dc)
        msched = [(0, 1, 2), (3, 4, 5), (6, 7), ()]
        for t in range(NTT):
            nxt = t + 1 < NTT
            if nxt:
                p3_x_loads(t + 1)
            if t + 2 < NTT:
                p3_og_loads(t + 2)
            for s_ in range(4):
                wa_step(t, s_)
                if s_ >= 1:
                    wb_step(t, s_ - 1)
                if nxt:
                    for dc in msched[s_]:
                        merge_step(t + 1, dc)
            wb_step(t, 3)
            p3_stores(t)
        cx.barrier()
        p3.close()
        pw3.close()

        p4 = ExitStack()
        wup = sb(p4, "wup", [128, 8, 4096], BF16)
        wdn = sb(p4, "wdn", [128, 32, D], BF16)
        b_wup = [Buf("wup%d" % i) for i in range(4)]
        b_wdn = [Buf("wdn%d" % i) for i in range(4)]
        sl_w4 = cx.slot()
        for i in range(4):
            cx.dma(pool, sl_w4, wup[:, :, 1024 * i:1024 * (i + 1)],
                   w_up_d[:, 1024 * i:1024 * (i + 1)].rearrange("(kc p) n -> p kc n", p=128), writes=[b_wup[i]])
        for i in range(4):
            cx.dma(pool, sl_w4, wdn[:, 8 * i:8 * (i + 1), :],
                   w_dn_d[1024 * i:1024 * (i + 1), :].rearrange("(c p) n -> p c n", p=128), writes=[b_wdn[i]])
        h2Tt = [sb(p4, "h2Tt%d" % i, [128, 8, TT], BF16) for i in range(2)]
        b_h2Tt = [Buf("h2Tt0"), Buf("h2Tt1")]
        sl_h2Tt = [cx.slot(), cx.slot()]
        x4 = sb(p4, "x4", [128, 4, D], F32)
        b_x4 = Buf("x4")
        sl_x4 = cx.slot()
        sl_y4 = cx.slot()
        upT = sb(p4, "upT", [128, 32, TT], BF16)
        b_upT = [Buf("upT%d" % c) for c in range(32)]
        rl = [sb(p4, "rl%d" % i, [128, TT], F32) for i in range(2)]
        b_rl = [Buf("rl0"), Buf("rl1")]
        upp = [ps(p4, "upp%d" % i, [128, 512], F32) for i in range(3)]
        b_upp = [Buf("upp%d" % i) for i in range(3)]
        acc4 = [ps(p4, "acc4_%d" % i, [128, 512], F32) for i in range(2)]
        b_acc4 = [Buf("acc4_0"), Buf("acc4_1")]

        def p4_hload(t):
            i = t % 2
            cx.dma(sp, sl_h2Tt[i], h2Tt[i][:], H2T[:, :, t * TT:(t + 1) * TT].rearrange("c p t -> p c t"),
                   writes=[b_h2Tt[i]])

        def p4_xload(t):
            cx.dma(sp, sl_x4, x4[:], y_d[t * TT:(t + 1) * TT, :].rearrange("(s p) d -> p s d", p=128), writes=[b_x4])

        p4_hload(0)
        p4_xload(0)
        uc = 0
        for t in range(NTT):
            if t + 1 < NTT:
                p4_hload(t + 1)
            hT_, bh = h2Tt[t % 2], b_h2Tt[t % 2]
            for c in range(32):
                U, bU = upp[uc % 3], b_upp[uc % 3]
                r_, br = rl[uc % 2], b_rl[uc % 2]
                uc += 1

                def mmu(h):
                    for kc in range(8):
                        ins = h.matmul(U[:, :], lhsT=wup[:, kc, c * 128:(c + 1) * 128], rhs=hT_[:, kc, :],
                                       start=(kc == 0), stop=(kc == 7))
                    return ins
                cx.op(pe, mmu, reads=[b_wup[c // 8], bh], writes=[bU])
                cx.op(act, lambda h: h.activation(out=r_[:], in_=U[:, :], func=AF.Relu), reads=[bU], writes=[br])
                cx.op(dve, lambda h: h.tensor_tensor(out=upT[:, c, :], in0=r_[:], in1=r_[:], op=ALU.mult),
                      reads=[br], writes=[b_upT[c]])
            for s in range(4):
                for half in range(2):
                    A, bA = acc4[half], b_acc4[half]

                    def mmd(h):
                        for c in range(32):
                            ins = h.matmul(A[:, :], lhsT=upT[:, c, s * 128:(s + 1) * 128],
                                           rhs=wdn[:, c, half * 512:(half + 1) * 512], start=(c == 0), stop=(c == 31))
                        return ins
                    cx.op(pe, mmd, reads=b_upT + b_wdn, writes=[bA])
                    cx.op(dve, lambda h: h.tensor_tensor(out=x4[:, s, half * 512:(half + 1) * 512], in0=A[:, :],
                                                         in1=x4[:, s, half * 512:(half + 1) * 512], op=ALU.add),
                          reads=[bA, b_x4], writes=[b_x4])
            cx.dma(sp, sl_y4, y_d[t * TT:(t + 1) * TT, :].rearrange("(s p) d -> p s d", p=128), x4[:], reads=[b_x4])
            if t + 1 < NTT:
                p4_xload(t + 1)
        cx.barrier()
        p4.close()
    return nc


def alibi(n):
    return (2.0 ** (-8.0 * (np.arange(n) + 1) / n)).astype(np.float64)


def make_tables(is_prompt):
    sa = alibi(8)
    a = np.arange(128)[:, None]
    b = np.arange(384)[None, :]
    diff = b - 128 - a
    wa = np.zeros((128, 3, 8, 384), np.float64)
    for h in range(8):
        base = np.where(np.abs(diff) <= 128, np.exp(-sa[h] * np.abs(diff)), 0.0)
        wa[:, 0, h] = base
        lo = base.copy()
        hi = base.copy()
        if is_prompt:
            lo[:, 256:] = 0.0
            hi[:, :128] = 0.0
        wa[:, 1, h] = lo
        wa[:, 2, h] = hi
    sb_ = alibi(12)
    b2 = np.arange(256)[None, :]
    diff2 = b2 - 64 - a
    wb = np.zeros((128, 3, 12, 256), np.float64)
    for g in range(3):
        for hh in range(4):
            j = g * 4 + hh
            base = np.where(np.abs(diff2) <= 64, np.exp(-sb_[j] * np.abs(diff2) * B_DIL[g]), 0.0)
            wb[:, 0, j] = base
            lo = base.copy()
            hi = base.copy()
            if is_prompt:
                lo[:, 192:] = 0.0
                hi[:, :64] = 0.0
            wb[:, 1, j] = lo
            wb[:, 2, j] = hi
    return wa.astype(NPBF), wb.astype(NPBF)


def make_params(g_mix, g_mem, g_mlp, b_gate, gq_a, gk_a, gq_b, gk_b, gq_m, gk_m, sink_a):
    p = np.zeros((128, P_N), np.float32)
    p[:, P_GMIX:P_GMIX + 8] = g_mix.reshape(8, 128).T
    p[:, P_GMEM:P_GMEM + 8] = g_mem.reshape(8, 128).T
    p[:, P_GMLP:P_GMLP + 8] = g_mlp.reshape(8, 128).T
    p[:, P_BG:P_BG + 24] = b_gate.reshape(24, 128).T
    p[:, P_GQA] = np.concatenate([gq_a.reshape(64)] * 2)
    p[:, P_GKA] = np.concatenate([gk_a.reshape(64)] * 2)
    p[:, P_GQB] = gq_b.reshape(128)
    p[:, P_GKB] = gk_b.reshape(128)
    p[:, P_GQM] = gq_m.reshape(128)
    p[:, P_GKM] = gk_m.reshape(128)
    s = sink_a.reshape(8)
    for c in range(4):
        p[0:64, P_SINK + c] = s[2 * c]
        p[64:128, P_SINK + c] = s[2 * c + 1]
    return p


def make_in_maps(x_prompt, x_sample, mem_prompt, mem_sample, g_mix, g_mem, w_in, b_gate, w_mem_kv,
                 gq_a, gk_a, sink_a, gq_b, gk_b, gq_m, gk_m, w_branch, w_out, g_mlp, w_up, w_down):
    f = lambda a: np.ascontiguousarray(np.asarray(a, dtype=np.float32))
    params = make_params(f(g_mix), f(g_mem), f(g_mlp), f(b_gate), f(gq_a), f(gk_a), f(gq_b), f(gk_b), f(gq_m),
                         f(gk_m), f(sink_a))
    ident = np.eye(128, dtype=np.float32).astype(NPBF)
    tabs = {True: make_tables(True), False: make_tables(False)}
    shared = dict(w_in=f(w_in).reshape(D, 8960), w_mem_kv=f(w_mem_kv).reshape(D, 1024),
                  w_branch=f(w_branch).reshape(1536, D), w_out=f(w_out).reshape(D, D),
                  w_up=f(w_up).reshape(D, 4096), w_down=f(w_down).reshape(4096, D),
                  params=params, ident=ident)
    xp, xs, mp, ms = f(x_prompt), f(x_sample), f(mem_prompt), f(mem_sample)
    in_maps = []
    for c in range(8):
        m = dict(shared)
        if c < 4:
            m["x"] = xp[2 * c:2 * c + 2].reshape(NT, D)
            m["mem"] = mp[2 * c:2 * c + 2].reshape(NMEM, D)
            wa, wb = tabs[True]
        else:
            m["x"] = xs[c - 4].reshape(NT, D)
            m["mem"] = np.concatenate([ms[c - 4], ms[c - 4]], axis=0)
            wa, wb = tabs[False]
        m["wa_tab"], m["wb_tab"] = wa, wb
        in_maps.append(m)
    return in_maps


_NC_CACHE = {}


def kernel(**inputs):
    in_maps = make_in_maps(**inputs)
    if "nc" not in _NC_CACHE:
        _NC_CACHE["nc"] = build_program()
    nc = _NC_CACHE["nc"]
    res = run_bass_kernel_spmd(nc, in_maps, core_ids=list(range(8)))
    ys = [np.asarray(r["y"], dtype=np.float32) for r in res.results]
    y_prompt = np.stack([ys[c].reshape(2, 2048, D) for c in range(4)], 0).reshape(8, 2048, D)
    y_sample = np.stack([ys[c] for c in range(4, 8)], 0)
    return (y_prompt, y_sample)
```
